# Optimizing a Trainium2 kernel written in Bass

```python
import math
import jax, jax.numpy as jnp
from jax import lax
import numpy as np

D_MODEL = 1024
BATCH = 8
SEQ = 2048
DEPTH = 2

CHUNK = 64
N_MIXERS = 2
N_RGLRU_LAYERS = (DEPTH + 1) // 2
N_RWKV_LAYERS = DEPTH // 2

RG_WIDTH = ((4 * D_MODEL // 3 + 127) // 128) * 128
RG_HEADS = 16
RG_BLOCK = RG_WIDTH // RG_HEADS
RG_CONV = 4
RG_C = 8.0

RWKV_HEAD = 64
RWKV_HEADS = D_MODEL // RWKV_HEAD
RWKV_DECAY_LORA = 64
RWKV_AAA_LORA = 64
RWKV_GATE_LORA = 128
RWKV_GN_EPS = 64e-5
RWKV_L2_EPS = 1e-12

PEER_HEADS = 8
PEER_NKEYS = 128
PEER_EXPERTS = PEER_NKEYS * PEER_NKEYS
PEER_DKEY = 256
PEER_DHALF = PEER_DKEY // 2
PEER_TOPK = 16
PEER_BLOCK = 128

DEEPNORM_ALPHA = (2 * DEPTH) ** 0.25
DEEPNORM_BETA = (8 * DEPTH) ** -0.25
LN_EPS = 1e-5

kernel_name = "hybrid_rglru_rwkv7_peer_deepnorm"


def layer_norm(x, g, b):
    xf = x.astype(jnp.float32)
    mu = jnp.mean(xf, axis=-1, keepdims=True)
    var = jnp.mean(jnp.square(xf - mu), axis=-1, keepdims=True)
    return ((xf - mu) * lax.rsqrt(var + LN_EPS) * g + b).astype(x.dtype)


def causal_depthwise_conv(x, w, b):
    y = lax.conv_general_dilated(
        x, w[:, None, :], window_strides=(1,), padding=[(RG_CONV - 1, 0)],
        dimension_numbers=("NWC", "WIO", "NWC"), feature_group_count=x.shape[-1])
    return y + b


def _linear_recurrence_combine(left, right):
    a_l, b_l = left
    a_r, b_r = right
    return a_l * a_r, a_r * b_l + b_r


def rglru_mixer(x, w_in, conv_w, conv_b, w_a, b_a, w_x, b_x, lam, w_out):
    bsz, s, _ = x.shape
    h = x @ w_in
    gate_branch = jax.nn.gelu(h[..., :RG_WIDTH])
    xc = causal_depthwise_conv(h[..., RG_WIDTH:], conv_w, conv_b)
    xh = xc.reshape(bsz, s, RG_HEADS, RG_BLOCK).astype(jnp.float32)
    r = jax.nn.sigmoid(jnp.einsum("bshi,hij->bshj", xh, w_a.astype(jnp.float32)) + b_a)
    i = jax.nn.sigmoid(jnp.einsum("bshi,hij->bshj", xh, w_x.astype(jnp.float32)) + b_x)
    log_a = -RG_C * jax.nn.softplus(-lam.astype(jnp.float32)) * r
    a = jnp.exp(log_a)
    u = jnp.sqrt(-jnp.expm1(2.0 * log_a)) * (i * xh)
    _, hs = lax.associative_scan(_linear_recurrence_combine, (a, u), axis=1)
    y = (hs.reshape(bsz, s, RG_WIDTH) * gate_branch).astype(x.dtype)
    return y @ w_out


def _rwkv7_step(state, inp):
    r_t, w_t, k_t, v_t, a_t, b_t = inp
    sa = jnp.einsum("bhvk,bhk->bhv", state, a_t)
    state = (state * w_t[:, :, None, :] + sa[..., None] * b_t[:, :, None, :]
             + v_t[..., None] * k_t[:, :, None, :])
    return state, jnp.einsum("bhvk,bhk->bhv", state, r_t)


def rwkv7_time_mix(x, mix, w_r, w_k, w_v, w0, w1, w2, a0, a1, a2, g1, g2,
                   k_k, k_a, r_k, lnx_g, lnx_b, w_o):
    bsz, s, d = x.shape
    xx = jnp.pad(x, ((0, 0), (1, 0), (0, 0)))[:, :-1] - x
    xr, xw, xk, xv, xa, xg = (x + xx * mix[m] for m in range(6))
    r = xr @ w_r
    k = xk @ w_k
    v = xv @ w_v
    w_log = -jax.nn.softplus(-(w0 + jnp.tanh(xw @ w1) @ w2)) - 0.5
    a = jax.nn.sigmoid(a0 + (xa @ a1) @ a2)
    g = jax.nn.sigmoid(xg @ g1) @ g2

    def heads(t):
        return t.reshape(bsz, s, RWKV_HEADS, RWKV_HEAD).astype(jnp.float32)

    r, w_log, k, v, a = heads(r), heads(w_log), heads(k), heads(v), heads(a)
    kk = k * k_k.reshape(RWKV_HEADS, RWKV_HEAD).astype(jnp.float32)
    kk = kk / jnp.maximum(jnp.sqrt(jnp.sum(kk * kk, axis=-1, keepdims=True)), RWKV_L2_EPS)
    k = k * (1.0 + (a - 1.0) * k_a.reshape(RWKV_HEADS, RWKV_HEAD).astype(jnp.float32))
    decay = jnp.exp(-jnp.exp(w_log))
    seq = tuple(jnp.moveaxis(t, 1, 0) for t in (r, decay, k, v, -kk, kk * a))
    state0 = jnp.zeros((bsz, RWKV_HEADS, RWKV_HEAD, RWKV_HEAD), jnp.float32)
    _, o = lax.scan(_rwkv7_step, state0, seq)
    o = jnp.moveaxis(o, 0, 1)
    mu = jnp.mean(o, axis=-1, keepdims=True)
    var = jnp.mean(jnp.square(o - mu), axis=-1, keepdims=True)
    o = ((o - mu) * lax.rsqrt(var + RWKV_GN_EPS)).reshape(bsz, s, d) * lnx_g + lnx_b
    bonus = jnp.sum(r * k * r_k.astype(jnp.float32), axis=-1, keepdims=True) * v
    o = o + bonus.reshape(bsz, s, d)
    return (o * g).astype(x.dtype) @ w_o


def peer_channel_mix(x, w_q, sub_keys, u, v):
    bsz, s, d = x.shape
    n_tok = bsz * s
    n_blk = n_tok // PEER_BLOCK
    xt = x.reshape(n_tok, d)
    q = (xt @ w_q).reshape(n_blk, PEER_BLOCK, PEER_HEADS, 2, PEER_DHALF)
    keys = sub_keys.astype(jnp.float32)

    def block(args):
        xb, qb = args
        sc = jnp.einsum("thpd,hpnd->thpn", qb.astype(jnp.float32), keys)
        sv, si = lax.top_k(sc, PEER_TOPK)
        comb = (sv[:, :, 0, :, None] + sv[:, :, 1, None, :]).reshape(
            PEER_BLOCK, PEER_HEADS, PEER_TOPK * PEER_TOPK)
        cv, ci = lax.top_k(comb, PEER_TOPK)
        i0 = jnp.take_along_axis(si[:, :, 0], ci // PEER_TOPK, axis=-1)
        i1 = jnp.take_along_axis(si[:, :, 1], ci % PEER_TOPK, axis=-1)
        eid = i0 * PEER_NKEYS + i1
        gate = jax.nn.softmax(cv, axis=-1)
        u_sel = u[eid]
        v_sel = v[eid]
        act = jax.nn.gelu(jnp.einsum("td,thkd->thk", xb, u_sel).astype(jnp.float32))
        coef = (gate * act).astype(xb.dtype)
        return jnp.einsum("thk,thkd->td", coef, v_sel)

    y = lax.map(block, (xt.reshape(n_blk, PEER_BLOCK, d), q))
    return y.reshape(bsz, s, d)


def setup_inputs(seed: int = 0) -> dict:
    key = jax.random.key(seed)
    ks = jax.random.split(key, 40)
    f32 = jnp.float32
    D, W, H, Bk = D_MODEL, RG_WIDTH, RG_HEADS, RG_BLOCK
    nA, nB = N_RGLRU_LAYERS, N_RWKV_LAYERS
    nrm = lambda k, shape, scale: jax.random.normal(k, shape, f32) * scale

    x = jax.random.normal(ks[0], (BATCH, SEQ, D), f32)
    rg_w_in = nrm(ks[1], (nA, D, 2 * W), D ** -0.5)
    rg_conv_w = nrm(ks[2], (nA, RG_CONV, W), RG_CONV ** -0.5)
    rg_conv_b = nrm(ks[3], (nA, W), 0.01)
    rg_w_a = nrm(ks[4], (nA, H, Bk, Bk), Bk ** -0.5)
    rg_b_a = nrm(ks[5], (nA, H, Bk), 0.01)
    rg_w_x = nrm(ks[6], (nA, H, Bk, Bk), Bk ** -0.5)
    rg_b_x = nrm(ks[7], (nA, H, Bk), 0.01)
    a_pow = jax.random.uniform(ks[8], (nA, H, Bk), f32, 0.9, 0.999) ** (1.0 / RG_C)
    rg_lambda = jnp.log(a_pow) - jnp.log1p(-a_pow)
    rg_w_out = nrm(ks[9], (nA, W, D), DEEPNORM_BETA * W ** -0.5)
    rw_mix = jax.random.uniform(ks[10], (nB, 6, D), f32)
    rw_w_r = nrm(ks[11], (nB, D, D), D ** -0.5)
    rw_w_k = nrm(ks[12], (nB, D, D), D ** -0.5)
    rw_w_v = nrm(ks[13], (nB, D, D), D ** -0.5)
    rw_w0 = jax.random.uniform(ks[14], (nB, D), f32, -5.0, -0.5)
    rw_w1 = nrm(ks[15], (nB, D, RWKV_DECAY_LORA), D ** -0.5)
    rw_w2 = nrm(ks[16], (nB, RWKV_DECAY_LORA, D), 0.1 * RWKV_DECAY_LORA ** -0.5)
    rw_a0 = nrm(ks[17], (nB, D), 0.1)
    rw_a1 = nrm(ks[18], (nB, D, RWKV_AAA_LORA), D ** -0.5)
    rw_a2 = nrm(ks[19], (nB, RWKV_AAA_LORA, D), 0.5 * RWKV_AAA_LORA ** -0.5)
    rw_g1 = nrm(ks[20], (nB, D, RWKV_GATE_LORA), D ** -0.5)
    rw_g2 = nrm(ks[21], (nB, RWKV_GATE_LORA, D), RWKV_GATE_LORA ** -0.5)
    rw_k_k = 0.85 + nrm(ks[22], (nB, D), 0.05)
    rw_k_a = 1.0 + nrm(ks[23], (nB, D), 0.05)
    rw_r_k = nrm(ks[24], (nB, RWKV_HEADS, RWKV_HEAD), 0.1)
    rw_lnx_g = 1.0 + nrm(ks[25], (nB, D), 0.01)
    rw_lnx_b = nrm(ks[26], (nB, D), 0.01)
    rw_w_o = nrm(ks[27], (nB, D, D), DEEPNORM_BETA * D ** -0.5)
    peer_w_q = nrm(ks[28], (DEPTH, D, PEER_HEADS * PEER_DKEY), D ** -0.5)
    peer_sub_keys = nrm(ks[29], (DEPTH, PEER_HEADS, 2, PEER_NKEYS, PEER_DHALF), PEER_DHALF ** -0.5)
    peer_u = nrm(ks[30], (DEPTH, PEER_EXPERTS, D), D ** -0.5)
    peer_v = nrm(ks[31], (DEPTH, PEER_EXPERTS, D), DEEPNORM_BETA * PEER_HEADS ** -0.5)
    ln_g = 1.0 + nrm(ks[32], (DEPTH, 2, D), 0.01)
    ln_b = nrm(ks[33], (DEPTH, 2, D), 0.01)
    return {
        "x": x,
        "rg_w_in": rg_w_in, "rg_conv_w": rg_conv_w, "rg_conv_b": rg_conv_b,
        "rg_w_a": rg_w_a, "rg_b_a": rg_b_a, "rg_w_x": rg_w_x, "rg_b_x": rg_b_x,
        "rg_lambda": rg_lambda, "rg_w_out": rg_w_out,
        "rw_mix": rw_mix, "rw_w_r": rw_w_r, "rw_w_k": rw_w_k, "rw_w_v": rw_w_v,
        "rw_w0": rw_w0, "rw_w1": rw_w1, "rw_w2": rw_w2,
        "rw_a0": rw_a0, "rw_a1": rw_a1, "rw_a2": rw_a2,
        "rw_g1": rw_g1, "rw_g2": rw_g2, "rw_k_k": rw_k_k, "rw_k_a": rw_k_a,
        "rw_r_k": rw_r_k, "rw_lnx_g": rw_lnx_g, "rw_lnx_b": rw_lnx_b, "rw_w_o": rw_w_o,
        "peer_w_q": peer_w_q, "peer_sub_keys": peer_sub_keys, "peer_u": peer_u, "peer_v": peer_v,
        "ln_g": ln_g, "ln_b": ln_b,
    }


def reference(x, rg_w_in, rg_conv_w, rg_conv_b, rg_w_a, rg_b_a, rg_w_x, rg_b_x, rg_lambda, rg_w_out,
              rw_mix, rw_w_r, rw_w_k, rw_w_v, rw_w0, rw_w1, rw_w2, rw_a0, rw_a1, rw_a2,
              rw_g1, rw_g2, rw_k_k, rw_k_a, rw_r_k, rw_lnx_g, rw_lnx_b, rw_w_o,
              peer_w_q, peer_sub_keys, peer_u, peer_v, ln_g, ln_b):
    for i in range(DEPTH):
        j = i // N_MIXERS
        if i % N_MIXERS == 0:
            m = rglru_mixer(x, rg_w_in[j], rg_conv_w[j], rg_conv_b[j], rg_w_a[j], rg_b_a[j],
                            rg_w_x[j], rg_b_x[j], rg_lambda[j], rg_w_out[j])
        else:
            m = rwkv7_time_mix(x, rw_mix[j], rw_w_r[j], rw_w_k[j], rw_w_v[j], rw_w0[j], rw_w1[j],
                               rw_w2[j], rw_a0[j], rw_a1[j], rw_a2[j], rw_g1[j], rw_g2[j],
                               rw_k_k[j], rw_k_a[j], rw_r_k[j], rw_lnx_g[j], rw_lnx_b[j], rw_w_o[j])
        x = layer_norm(DEEPNORM_ALPHA * x + m, ln_g[i, 0], ln_b[i, 0])
        c = peer_channel_mix(x, peer_w_q[i], peer_sub_keys[i], peer_u[i], peer_v[i])
        x = layer_norm(DEEPNORM_ALPHA * x + c, ln_g[i, 1], ln_b[i, 1])
    return x
```

```python
import math
from contextlib import ExitStack
import numpy as np
import concourse.bass as bass
import concourse.mybir as mybir
from concourse.bass_utils import run_bass_kernel_spmd

F32 = mybir.dt.float32
BF16 = mybir.dt.bfloat16
U32 = mybir.dt.uint32
I32 = mybir.dt.int32
AF = mybir.ActivationFunctionType
ALU = mybir.AluOpType
AX = mybir.AxisListType

D = 1024
T = 2048
NCORES = 8
RG_W = 1408
RG_H = 16
RG_B = 88
NWC = 11
ALPHA = 4.0 ** 0.25
LN_EPS = 1e-5
GELU_K = 2.0 * math.sqrt(2.0 / math.pi)

ENG = {'pe': 'tensor', 'dve': 'vector', 'act': 'scalar', 'pool': 'gpsimd', 'sp': 'sync'}
NPOOL = 32


class Res:
    __slots__ = ('name', 'w', 'rs')

    def __init__(self, name):
        self.name = name
        self.w = None
        self.rs = []


class Op:
    __slots__ = ('eng', 'fn', 'deps', 'signal', 'is_dma', 'sem', 'val', 'semname')

    def __init__(self, eng, fn, is_dma):
        self.eng = eng
        self.fn = fn
        self.deps = []
        self.signal = False
        self.is_dma = is_dma
        self.sem = None
        self.val = 0
        self.semname = None


class Prog:
    def __init__(self, nc, st):
        self.nc = nc
        self.ops = []
        self.res = []
        self.sems = {e: st.enter_context(nc.semaphore('s_' + e)) for e in ENG}
        self.cnt = {e: 0 for e in ENG}
        self.dsem = {q: [st.enter_context(nc.semaphore('d_%s_%d' % (q, i))) for i in range(NPOOL)]
                     for q in ('sp', 'pool', 'act')}
        self.duse = {q: [0] * NPOOL for q in self.dsem}
        self.dlast = {q: [None] * NPOOL for q in self.dsem}
        self.drr = {q: 0 for q in self.dsem}
        self.waited = {e: {} for e in ENG}
        self.nblk = 0

    def R(self, name):
        r = Res(name)
        self.res.append(r)
        return r

    def Rs(self, name, n):
        return [self.R('%s%d' % (name, i)) for i in range(n)]

    def op(self, eng, fn, reads=(), writes=(), dma=False):
        o = Op(eng, fn, dma)
        deps = []
        for r in reads:
            if r.w is not None:
                deps.append(r.w)
        for w in writes:
            if w.w is not None:
                deps.append(w.w)
            deps.extend(w.rs)
        if dma:
            q = eng
            j = self.drr[q]
            self.drr[q] = (j + 1) % NPOOL
            if self.dlast[q][j] is not None:
                deps.append(self.dlast[q][j])
            self.duse[q][j] += 1
            o.sem = self.dsem[q][j]
            o.semname = 'd_%s_%d' % (q, j)
            o.val = 16 * self.duse[q][j]
            o.signal = True
            self.dlast[q][j] = o
        seen = set()
        for d in deps:
            if d is o or id(d) in seen:
                continue
            seen.add(id(d))
            if (not d.is_dma) and (not dma) and d.eng == 'pe' and eng == 'pe':
                continue
            d.signal = True
            o.deps.append(d)
        for r in reads:
            r.rs.append(o)
        for w in writes:
            w.w = o
            w.rs = []
        self.ops.append(o)
        return o

    def dma(self, out, in_, reads=(), writes=(), q='sp'):
        return self.op(q, lambda e: e.dma_start(out=out, in_=in_), reads, writes, dma=True)

    def mm(self, out, lhsT, rhs, start, stop, reads=(), writes=()):
        return self.op('pe', lambda e: e.matmul(out, lhsT, rhs, start=start, stop=stop), reads, writes)

    def tr(self, out, in_, ident, reads=(), writes=()):
        return self.op('pe', lambda e: e.transpose(out, in_, ident), reads, writes)

    def act(self, out, in_, func, reads=(), writes=(), bias=None, scale=None):
        kw = {}
        if bias is not None:
            kw['bias'] = bias
        if scale is not None:
            kw['scale'] = scale
        return self.op('act', lambda e: e.activation(out, in_, func, **kw), reads, writes)

    def tt(self, out, in0, in1, op, reads=(), writes=(), eng='dve'):
        return self.op(eng, lambda e: e.tensor_tensor(out, in0, in1, op), reads, writes)

    def ts(self, out, in0, s1, s2, op0, op1=None, reads=(), writes=(), eng='dve'):
        if op1 is None:
            return self.op(eng, lambda e: e.tensor_scalar(out, in0, s1, None, op0), reads, writes)
        return self.op(eng, lambda e: e.tensor_scalar(out, in0, s1, s2, op0, op1), reads, writes)

    def stt(self, out, in0, scalar, in1, op0, op1, reads=(), writes=()):
        return self.op('dve', lambda e: e.scalar_tensor_tensor(out, in0, scalar, in1, op0, op1), reads, writes)

    def cp(self, out, in_, reads=(), writes=(), eng='dve'):
        if eng == 'act':
            return self.op('act', lambda e: e.copy(out, in_), reads, writes)
        return self.op(eng, lambda e: e.tensor_copy(out, in_), reads, writes)

    def flush(self):
        last = {}
        for o in self.ops:
            last[o.eng if not o.is_dma else ('dma', o.semname)] = o
        tails = []
        for k, o in last.items():
            o.signal = True
            tails.append(o)
        used = sorted({o.eng for o in self.ops})
        for e in used:
            b = Op(e, None, False)
            b.deps = [t for t in tails]
            self.ops.append(b)
        for o in self.ops:
            if o.is_dma or o.fn is None:
                continue
            if o.signal:
                self.cnt[o.eng] += 1
                o.sem = self.sems[o.eng]
                o.semname = 's_' + o.eng
                o.val = self.cnt[o.eng]
        nc = self.nc
        ops = self.ops
        with nc.Block() as block:
            for ename, attr in ENG.items():
                eops = [o for o in ops if o.eng == ename]
                if not eops:
                    continue

                def body(engine, eops=eops, ename=ename):
                    wt = self.waited[ename]
                    for o in eops:
                        for d in o.deps:
                            if wt.get(d.semname, 0) < d.val:
                                engine.wait_ge(d.sem, d.val)
                                wt[d.semname] = d.val
                        if o.fn is not None:
                            inst = o.fn(engine)
                            if o.is_dma:
                                inst.then_inc(o.sem, 16)
                            elif o.signal:
                                inst.then_inc(o.sem, 1)

                getattr(block, attr)(body)
        self.ops = []
        for r in self.res:
            r.w = None
            r.rs = []
        self.dlast = {q: [None] * NPOOL for q in self.dsem}
        self.nblk += 1


class Ctx:
    pass


def make_ctx(nc, st):
    C = Ctx()
    C.nc = nc
    C.P = Prog(nc, st)
    C.ps = [st.enter_context(nc.psum_tensor('ps%d' % i, [128, 512], F32)) for i in range(8)]
    C.psr = C.P.Rs('psr', 8)
    return C


_uid = [0]


def sbuf(nc, st, name, shape, dt=F32):
    _uid[0] += 1
    return st.enter_context(nc.sbuf_tensor('sb%d_%s' % (_uid[0], name), shape, dt))


def ln_store(C, L, z, zr, out_ap, pp=128):
    P = C.P
    st6, mv, sd = L['st6'], L['mv'], L['sd']
    r6, rmv, rsd = L['r6'], L['rmv'], L['rsd']
    for h in range(2):
        P.op('dve', lambda e, h=h: e.bn_stats(st6[:pp, h, :], z[:pp, h * 512:(h + 1) * 512]), [zr], [r6])
    P.op('dve', lambda e: e.bn_aggr(mv[:pp, :], st6[:pp, :, :].rearrange('p a b -> p (a b)')), [r6], [rmv])
    P.ts(sd[:pp, 0:1], mv[:pp, 1:2], LN_EPS, None, ALU.add, None, [rmv], [rsd])
    P.act(sd[:pp, 0:1], sd[:pp, 0:1], AF.Sqrt, [rsd], [rsd])
    P.op('dve', lambda e: e.reciprocal(sd[:pp, 0:1], sd[:pp, 0:1]), [rsd], [rsd])
    P.ts(z[:pp, :], z[:pp, :], mv[:pp, 0:1], sd[:pp, 0:1], ALU.subtract, ALU.mult, [zr, rmv, rsd], [zr])
    P.tt(z[:pp, :], z[:pp, :], L['g'][:pp, :], ALU.mult, [zr, L['rg']], [zr])
    P.tt(z[:pp, :], z[:pp, :], L['b'][:pp, :], ALU.add, [zr, L['rg']], [zr])
    P.dma(out_ap, z[:pp, :], [zr], [L['rout']], q=L.get('q', 'sp'))


def ln_setup(C, st, g_d, b_d, tag):
    nc, P = C.nc, C.P
    L = {}
    L['g'] = sbuf(nc, st, 'lng' + tag, [128, 1024])
    L['b'] = sbuf(nc, st, 'lnb' + tag, [128, 1024])
    L['st6'] = sbuf(nc, st, 'lnst' + tag, [128, 2, 6])
    L['mv'] = sbuf(nc, st, 'lnmv' + tag, [128, 2])
    L['sd'] = sbuf(nc, st, 'lnsd' + tag, [128, 2])
    L['rg'] = P.R('lnrg')
    L['r6'] = P.R('lnr6')
    L['rmv'] = P.R('lnrmv')
    L['rsd'] = P.R('lnrsd')
    L['rout'] = P.R('lnrout')
    P.dma(L['g'][:, :], g_d, [], [L['rg']])
    P.dma(L['b'][:, :], b_d, [], [L['rg']])
    return L


def gelu_tanh(P, x, xr, t, tr_, eng2='dve'):
    P.act(t, x, AF.Square, [xr], [tr_])
    P.ts(t, t, 0.044715, 1.0, ALU.mult, ALU.add, [tr_], [tr_])
    P.tt(t, t, x, ALU.mult, [tr_, xr], [tr_])
    P.act(t, t, AF.Sigmoid, [tr_], [tr_], scale=GELU_K)
    P.tt(x, x, t, ALU.mult, [xr, tr_], [xr])


def phase_rg(C, x_in, x_out, W):
    nc, P = C.nc, C.P
    SEG = 256
    NSEG = T // SEG
    with ExitStack() as st:
        ident = sbuf(nc, st, 'ident', [128, 128])
        wout = sbuf(nc, st, 'wout', [128, NWC, 1024], BF16)
        wstg = sbuf(nc, st, 'wstg', [128, 1024])
        r_wstg = P.R('wstg')
        waB = sbuf(nc, st, 'waB', [128, NWC, 3, 128])
        wxB = sbuf(nc, st, 'wxB', [128, NWC, 3, 128])
        cst = sbuf(nc, st, 'cst', [128, 9, NWC])
        wbuf = [sbuf(nc, st, 'wbuf%d' % i, [128, 8, 128]) for i in range(3)]
        wbb = [sbuf(nc, st, 'wbb%d' % i, [128, 8, 128], BF16) for i in range(3)]
        r_wbb = P.Rs('wbb', 3)
        xtm = [sbuf(nc, st, 'xtm%d' % i, [128, 1024]) for i in range(2)]
        xT = sbuf(nc, st, 'xT', [128, 8, SEG], BF16)
        hgb = sbuf(nc, st, 'hgb', [128, NWC, SEG], BF16)
        r_hgb = P.R('hgb')
        hg = sbuf(nc, st, 'hg', [128, NWC, SEG])
        xcp = sbuf(nc, st, 'xcp', [128, NWC, SEG + 3])
        xc = sbuf(nc, st, 'xc', [128, NWC, SEG])
        rr = sbuf(nc, st, 'rr', [128, NWC, SEG])
        ii = sbuf(nc, st, 'ii', [128, NWC, SEG])
        aa = sbuf(nc, st, 'aa', [128, NWC, SEG])
        hs = sbuf(nc, st, 'hs', [128, NWC, SEG])
        zt = [sbuf(nc, st, 'zt%d' % i, [128, 1024]) for i in range(2)]
        L = ln_setup(C, st, W['ln_g'], W['ln_b'], 'rg')
        L['q'] = 'pool'
        r_const = P.R('const')
        r_wbuf = P.Rs('wbuf', 3)
        r_xtm = P.Rs('xtm', 2)
        r_xT, r_hg, r_xcp, r_xc, r_rr, r_ii, r_aa, r_hs = [P.R(n) for n in
                                                          ('xT', 'hg', 'xcp', 'xc', 'rr', 'ii', 'aa', 'hs')]
        r_zt = P.Rs('zt', 2)
        ps, psr = C.ps, C.psr

        P.dma(ident[:, :], W['ident'], [], [r_const])
        for c in range(NWC):
            P.dma(wstg[:, :], W['rg_w_out'][:, c, :], [], [r_wstg])
            P.cp(wout[:, c, :], wstg[:, :], [r_wstg], [r_const], eng=('act' if c % 2 else 'dve'))
        P.dma(waB[:, :, :, :], W['rg_wa'], [], [r_const])
        P.dma(wxB[:, :, :, :], W['rg_wx'], [], [r_const])
        P.dma(cst[:, :, :], W['rg_cst'], [], [r_const])
        P.act(cst[:, 7, :], cst[:, 7, :], AF.Exp, [r_const], [r_const], scale=-1.0)
        P.act(cst[:, 7, :], cst[:, 7, :], AF.Ln, [r_const], [r_const], bias=1.0)
        P.ts(cst[:, 8, :], cst[:, 7, :], -16.0, None, ALU.mult, None, [r_const], [r_const])
        P.ts(cst[:, 7, :], cst[:, 7, :], -8.0, None, ALU.mult, None, [r_const], [r_const])
        P.op('dve', lambda e: e.memset(xcp[:, :, 0:3], 0.0), [], [r_xcp])
        P.op('dve', lambda e: e.memset(hs[:, :, SEG - 1:SEG], 0.0), [], [r_hs])

        pi = 0

        def nb():
            nonlocal pi
            b = pi
            pi = (pi + 1) % 8
            return b

        for s in range(NSEG):
            t0 = s * SEG
            for tt in range(2):
                P.dma(xtm[tt][:, :], x_in[t0 + tt * 128:t0 + (tt + 1) * 128, :], [], [r_xtm[tt]])
                for half in range(2):
                    b = nb()
                    for j in range(4):
                        k = half * 4 + j
                        P.tr(ps[b][:, j * 128:(j + 1) * 128], xtm[tt][:, k * 128:(k + 1) * 128], ident[:, :],
                             [r_xtm[tt], r_const], [psr[b]])
                    P.cp(xT[:, half * 4:half * 4 + 4, tt * 128:(tt + 1) * 128],
                         ps[b][:, :].rearrange('p (a b) -> p a b', a=4), [psr[b]], [r_xT], eng='act')
            for oc in range(2 * NWC):
                wb = oc % 3
                P.dma(wbuf[wb][:, :, :], W['rg_w_in'][oc], [], [r_wbuf[wb]])
                P.cp(wbb[wb][:, :, :], wbuf[wb][:, :, :], [r_wbuf[wb]], [r_wbb[wb]], eng=('act' if oc % 2 else 'dve'))
                b = nb()
                for k in range(8):
                    P.mm(ps[b][:, 0:SEG], wbb[wb][:, k, :], xT[:, k, :], k == 0, k == 7,
                         [r_wbb[wb], r_xT], [psr[b]])
                if oc < NWC:
                    P.cp(hg[:, oc, :], ps[b][:, 0:SEG], [psr[b]], [r_hg], eng='act')
                else:
                    P.cp(xcp[:, oc - NWC, 3:3 + SEG], ps[b][:, 0:SEG], [psr[b]], [r_xcp], eng='act')
            for c in range(NWC):
                P.ts(xc[:, c, :], xcp[:, c, 0:SEG], cst[:, 0, c:c + 1], cst[:, 4, c:c + 1], ALU.mult, ALU.add,
                     [r_xcp, r_const], [r_xc])
                for j in range(1, 4):
                    P.stt(xc[:, c, :], xcp[:, c, j:j + SEG], cst[:, j, c:c + 1], xc[:, c, :], ALU.mult, ALU.add,
                          [r_xcp, r_const, r_xc], [r_xc])
            P.cp(xcp[:, :, 0:3], xcp[:, :, SEG:SEG + 3], [r_xcp, r_xc], [r_xcp], eng='act')
            for (wB, bias_i, dst, rdst) in ((waB, 5, rr, r_rr), (wxB, 6, ii, r_ii)):
                for co in range(NWC):
                    b = nb()
                    cis = [ci for ci in (co - 1, co, co + 1) if 0 <= ci < NWC]
                    for n, ci in enumerate(cis):
                        P.mm(ps[b][:, 0:SEG], wB[:, co, ci - co + 1, :], xc[:, ci, :], n == 0, n == len(cis) - 1,
                             [r_const, r_xc], [psr[b]])
                    P.act(dst[:, co, :], ps[b][:, 0:SEG], AF.Sigmoid, [psr[b], r_const], [rdst],
                          bias=cst[:, bias_i, co:co + 1])
            for c in range(NWC):
                P.act(aa[:, c, :], rr[:, c, :], AF.Exp, [r_rr, r_const], [r_aa], scale=cst[:, 7, c:c + 1])
                P.act(rr[:, c, :], rr[:, c, :], AF.Exp, [r_rr, r_const], [r_rr], scale=cst[:, 8, c:c + 1])
            P.ts(rr[:, :, :], rr[:, :, :], -1.0, 1.0, ALU.mult, ALU.add, [r_rr], [r_rr])
            P.ts(rr[:, :, :], rr[:, :, :], 0.0, None, ALU.max, None, [r_rr], [r_rr])
            P.act(rr[:, :, :], rr[:, :, :], AF.Sqrt, [r_rr], [r_rr])
            P.tt(ii[:, :, :], ii[:, :, :], xc[:, :, :], ALU.mult, [r_ii, r_xc], [r_ii])
            P.tt(ii[:, :, :], ii[:, :, :], rr[:, :, :], ALU.mult, [r_ii, r_rr], [r_ii])
            P.cp(xc[:, :, 0:1], hs[:, :, SEG - 1:SEG], [r_hs, r_ii], [r_xc], eng='dve')
            for c in range(NWC):
                P.op('dve', lambda e, c=c: e.tensor_tensor_scan(hs[:, c, :], aa[:, c, :], ii[:, c, :],
                                                                 xc[:, c, 0:1], ALU.mult, ALU.add),
                     [r_aa, r_ii, r_xc], [r_hs])
            gelu_tanh(P, hg[:, :, :], r_hg, rr[:, :, :], r_rr)
            P.tt(hgb[:, :, :], hg[:, :, :], hs[:, :, :], ALU.mult, [r_hg, r_hs], [r_hgb])
            for tt in range(2):
                bs = [nb(), nb()]
                for half in range(2):
                    for c in range(NWC):
                        P.mm(ps[bs[half]][:, :], hgb[:, c, tt * 128:(tt + 1) * 128],
                             wout[:, c, half * 512:(half + 1) * 512], c == 0, c == NWC - 1,
                             [r_hgb, r_const], [psr[bs[half]]])
                for half in range(2):
                    P.stt(zt[tt][:, half * 512:(half + 1) * 512], xtm[tt][:, half * 512:(half + 1) * 512], ALPHA,
                          ps[bs[half]][:, :], ALU.mult, ALU.add, [r_xtm[tt], psr[bs[half]]], [r_zt[tt]])
                ln_store(C, L, zt[tt], r_zt[tt], x_out[t0 + tt * 128:t0 + (tt + 1) * 128, :])
        P.flush()


def _vec_pc(v, nchunk):
    return np.ascontiguousarray(v.reshape(nchunk, 128).T)


def rg_host_layout(inp):
    f = np.float32
    w_in = inp['rg_w_in'][0]
    w_in_l = np.ascontiguousarray(w_in.reshape(8, 128, 22, 128).transpose(2, 1, 0, 3))
    w_out_l = np.ascontiguousarray(inp['rg_w_out'][0].reshape(NWC, 128, 1024).transpose(1, 0, 2))

    def band(w):
        full = np.zeros((RG_W + 256, RG_W + 256), f)
        for h in range(RG_H):
            full[128 + h * RG_B:128 + (h + 1) * RG_B, 128 + h * RG_B:128 + (h + 1) * RG_B] = w[h]
        out = np.zeros((128, NWC, 3, 128), f)
        for co in range(NWC):
            for kk in range(3):
                ci = co - 1 + kk
                out[:, co, kk, :] = full[128 + ci * 128:128 + (ci + 1) * 128, 128 + co * 128:128 + (co + 1) * 128]
        return out

    cst = np.zeros((128, 9, NWC), f)
    for j in range(4):
        cst[:, j, :] = _vec_pc(inp['rg_conv_w'][0, j], NWC)
    cst[:, 4, :] = _vec_pc(inp['rg_conv_b'][0], NWC)
    cst[:, 5, :] = _vec_pc(inp['rg_b_a'][0].reshape(-1), NWC)
    cst[:, 6, :] = _vec_pc(inp['rg_b_x'][0].reshape(-1), NWC)
    cst[:, 7, :] = _vec_pc(inp['rg_lambda'][0].reshape(-1), NWC)
    return {'rg_w_in': w_in_l, 'rg_w_out': w_out_l, 'rg_wa': band(inp['rg_w_a'][0]),
            'rg_wx': band(inp['rg_w_x'][0]), 'rg_cst': cst}


def bc128(v):
    return np.ascontiguousarray(np.broadcast_to(v.reshape(1, -1), (128, v.size))).astype(np.float32)


NEG = -1.0e30
SPLIT_DOTS = True
DBG = {'nogather': False, 'nodve': False, 'halfrow': False}


def top16_gen(P, items):
    for (vals, rv, scratch, rscr, outv, outi, rout) in items:
        P.op('dve', lambda e, outv=outv, vals=vals: e.max(out=outv[:, 0:8], in_=vals), [rv], [rout])
        yield
    for (vals, rv, scratch, rscr, outv, outi, rout) in items:
        P.op('dve', lambda e, outv=outv, outi=outi, vals=vals: e.max_index(out=outi[:, 0:8], in_max=outv[:, 0:8],
                                                                         in_values=vals), [rv, rout], [rout])
        yield
    for (vals, rv, scratch, rscr, outv, outi, rout) in items:
        P.op('dve', lambda e, outv=outv, vals=vals, scratch=scratch: e.match_replace(
            out=scratch, in_to_replace=outv[:, 0:8], in_values=vals, imm_value=NEG), [rv, rout], [rscr])
        yield
    for (vals, rv, scratch, rscr, outv, outi, rout) in items:
        P.op('dve', lambda e, outv=outv, scratch=scratch: e.max(out=outv[:, 8:16], in_=scratch), [rscr], [rout])
        yield
    for (vals, rv, scratch, rscr, outv, outi, rout) in items:
        P.op('dve', lambda e, outv=outv, outi=outi, scratch=scratch: e.max_index(
            out=outi[:, 8:16], in_max=outv[:, 8:16], in_values=scratch), [rscr, rout], [rout])
        yield


def top16_batch(P, items):
    for (vals, rv, scratch, rscr, outv, outi, rout) in items:
        P.op('dve', lambda e, outv=outv, vals=vals: e.max(out=outv[:, 0:8], in_=vals), [rv], [rout])
    for (vals, rv, scratch, rscr, outv, outi, rout) in items:
        P.op('dve', lambda e, outv=outv, outi=outi, vals=vals: e.max_index(out=outi[:, 0:8], in_max=outv[:, 0:8],
                                                                         in_values=vals), [rv, rout], [rout])
    for (vals, rv, scratch, rscr, outv, outi, rout) in items:
        P.op('dve', lambda e, outv=outv, vals=vals, scratch=scratch: e.match_replace(
            out=scratch, in_to_replace=outv[:, 0:8], in_values=vals, imm_value=NEG), [rv, rout], [rscr])
    for (vals, rv, scratch, rscr, outv, outi, rout) in items:
        P.op('dve', lambda e, outv=outv, scratch=scratch: e.max(out=outv[:, 8:16], in_=scratch), [rscr], [rout])
    for (vals, rv, scratch, rscr, outv, outi, rout) in items:
        P.op('dve', lambda e, outv=outv, outi=outi, scratch=scratch: e.max_index(
            out=outi[:, 8:16], in_max=outv[:, 8:16], in_values=scratch), [rscr, rout], [rout])


def phase_peer(C, x_in, x_out, W, qT_d, uvb_d, tag):
    nc, P = C.nc, C.P
    ps, psr = C.ps, C.psr
    NT = T // 128
    pi = 0

    def nb():
        nonlocal pi
        b = pi
        pi = (pi + 1) % 8
        return b

    with ExitStack() as st:
        ident = sbuf(nc, st, 'ident', [128, 128])
        xT = sbuf(nc, st, 'xT', [128, 8, T])
        xl = [sbuf(nc, st, 'xl%d' % i, [128, 1024]) for i in range(2)]
        wq = [sbuf(nc, st, 'wq%d' % i, [128, 8, 128]) for i in range(2)]
        qs = [sbuf(nc, st, 'qs%d' % i, [128, T]) for i in range(2)]
        r_c = P.R('qc')
        r_xT = P.R('qxT')
        r_xl = P.Rs('qxl', 2)
        r_wq = P.Rs('qwq', 2)
        r_qs = P.Rs('qqs', 2)
        r_qd = P.R('qTd')
        P.dma(ident[:, :], W['ident'], [], [r_c])
        for tt in range(NT):
            s = tt % 2
            P.dma(xl[s][:, :], x_in[tt * 128:(tt + 1) * 128, :], [], [r_xl[s]])
            for half in range(2):
                b = nb()
                for j in range(4):
                    k = half * 4 + j
                    P.tr(ps[b][:, j * 128:(j + 1) * 128], xl[s][:, k * 128:(k + 1) * 128], ident[:, :],
                         [r_xl[s], r_c], [psr[b]])
                P.cp(xT[:, half * 4:half * 4 + 4, tt * 128:(tt + 1) * 128],
                     ps[b][:, :].rearrange('p (a b) -> p a b', a=4), [psr[b]], [r_xT],
                     eng=('act' if half else 'dve'))
        cf = [sbuf(nc, st, 'cf%d' % i, [128, 4096]) for i in range(2)]
        cb = [sbuf(nc, st, 'cb%d' % i, [128, 4096], BF16) for i in range(2)]
        r_cf, r_cb, r_uvb = P.Rs('qcf', 2), P.Rs('qcb', 2), P.R('quvb')
        NCAST = 16384 // 256
        cast_i = [0]

        def cast_chunk():
            i = cast_i[0]
            if i >= NCAST:
                return
            cast_i[0] += 1
            s2 = i % 2
            P.dma(cf[s2][:, :].rearrange('p (r c) -> p r c', r=2),
                  W['uv'][i * 256:(i + 1) * 256, :].rearrange('(p r) c -> p r c', r=2), [], [r_cf[s2]], q='act')
            P.cp(cb[s2][:, :], cf[s2][:, :], [r_cf[s2]], [r_cb[s2]], eng='dve')
            P.dma(uvb_d[i * 256:(i + 1) * 256, :].rearrange('(p r) c -> p r c', r=2),
                  cb[s2][:, :].rearrange('p (r c) -> p r c', r=2), [r_cb[s2]], [r_uvb], q='act')

        for hp in range(16):
            s = hp % 2
            for _ in range(4):
                cast_chunk()
            P.dma(wq[s][:, :, :], W['wq'][hp], [], [r_wq[s]])
            for tc in range(4):
                b = nb()
                for k in range(8):
                    P.mm(ps[b][:, :], wq[s][:, k, :], xT[:, k, tc * 512:(tc + 1) * 512], k == 0, k == 7,
                         [r_wq[s], r_xT], [psr[b]])
                P.cp(qs[s][:, tc * 512:(tc + 1) * 512], ps[b][:, :], [psr[b]], [r_qs[s]],
                     eng=('act' if tc % 2 else 'dve'))
            P.dma(qT_d[hp], qs[s][:, :], [r_qs[s]], [r_qd])
        P.flush()

    with ExitStack() as st:
        NS = 24
        NG = 8
        ident = sbuf(nc, st, 'identT', [128, 128])
        keysT = sbuf(nc, st, 'keysT', [128, 16, 128])
        iota16 = sbuf(nc, st, 'iota16', [128, 16])
        xt = [sbuf(nc, st, 'xt%d' % i, [128, 1024]) for i in range(2)]
        xb = [sbuf(nc, st, 'xb%d' % i, [128, 1024], BF16) for i in range(2)]
        qt = sbuf(nc, st, 'qt', [128, 16, 128])
        sc = sbuf(nc, st, 'sc', [128, 16, 128])
        scr8 = sbuf(nc, st, 'scr8', [128, 2048])
        sc2 = scr8[:, :].rearrange('p (a b) -> p a b', a=16)
        comb2 = scr8[:, :].rearrange('p (h i j) -> p h i j', h=8, i=16)
        eq = comb2
        sv = sbuf(nc, st, 'sv', [128, 16, 16])
        si = sbuf(nc, st, 'si', [128, 16, 16], U32)
        sif = sbuf(nc, st, 'sif', [128, 16, 16])
        comb = sbuf(nc, st, 'comb', [128, 8, 16, 16])
        cv = sbuf(nc, st, 'cv', [128, 8, 16])
        ci = sbuf(nc, st, 'ci', [128, 8, 16], U32)
        cq = sbuf(nc, st, 'cq', [128, 2, 8, 16], I32)
        cqf = sbuf(nc, st, 'cqf', [128, 2, 8, 16])
        idx = sbuf(nc, st, 'idx', [128, 2, 128])
        eidf = sbuf(nc, st, 'eidf', [128, 128])
        eid = [sbuf(nc, st, 'eid%d' % i, [128, 128], I32) for i in range(2)]
        gate = [sbuf(nc, st, 'gate%d' % i, [128, 8, 16]) for i in range(2)]
        gsum = sbuf(nc, st, 'gsum', [128, 8])
        actv = [sbuf(nc, st, 'actv%d' % i, [128, 128]) for i in range(2)]
        gtmp = [sbuf(nc, st, 'gtmp%d' % i, [128, 128]) for i in range(2)]
        coef = [sbuf(nc, st, 'coef%d' % i, [128, 128]) for i in range(2)]
        NJ = 4
        junks = [sbuf(nc, st, 'junk%d' % i, [128, 1024], BF16) for i in range(NJ)]
        r_junk = P.Rs('junk', NJ)
        jk_ctr = [0]
        uvg = [sbuf(nc, st, 'uvg%d' % i, [128, 2048], BF16) for i in range(NS)]
        NDG = 8
        dg = [sbuf(nc, st, 'dg%d' % i, [128, 128], BF16) for i in range(NDG)]
        zt = [sbuf(nc, st, 'ztp%d' % i, [128, 1024]) for i in range(2)]
        L = ln_setup(C, st, W['ln_g'], W['ln_b'], 'pe' + tag)
        r_c = P.R('tc')
        r_xt = P.Rs('txt', 2)
        r_xb = P.Rs('txb', 2)
        r_qt = P.R('tqt')
        r_sc, r_scr, r_sif, r_comb, r_cq, r_idx, r_eidf, r_gsum = \
            [P.R(n) for n in ('sc', 'scr', 'sif', 'comb', 'cq', 'idx', 'eidf', 'gsum')]
        r_svs = P.Rs('svs', 16)
        r_sc2s = P.Rs('sc2s', 16)
        r_cvs = P.Rs('cvs', 8)
        r_eid = P.Rs('eid', 2)
        r_gate = P.Rs('gate', 2)
        r_act = [P.Rs('act%d_' % i, 128 // NG) for i in range(2)]
        r_gt = [P.Rs('gt%d_' % i, 128 // NG) for i in range(2)]
        r_coef = [P.Rs('coef%d_' % i, 128 // NG) for i in range(2)]
        r_uv = P.Rs('uv', NS)
        r_dg = P.Rs('dg', NDG)
        r_z = P.Rs('z', 2)
        r_qd = P.R('qTd2')
        P.dma(ident[:, :], W['ident'], [], [r_c])
        P.dma(keysT[:, :, :], W['keysT'], [], [r_c])
        P.dma(iota16[:, :], W['iota16'], [], [r_c])
        svr = sv[:, :, :].rearrange('p (h two) i -> p h two i', two=2)
        sifr = sif[:, :, :].rearrange('p (h two) i -> p h two i', two=2)
        pj = 0

        def nb4():
            nonlocal pj
            b = pj
            pj = (pj + 1) % 4
            return b

        def routing(tt):
            s = tt % 2
            P.dma(xt[s][:, :], x_in[tt * 128:(tt + 1) * 128, :], [], [r_xt[s]])
            yield
            P.dma(qt[:, :, :], qT_d[:, :, tt * 128:(tt + 1) * 128].rearrange('h p t -> p h t'), [r_qd], [r_qt])
            yield
            P.cp(xb[s][:, :], xt[s][:, :], [r_xt[s]], [r_xb[s]], eng='act')
            yield
            for g4 in range(4):
                b = nb4()
                yield
                for j in range(4):
                    hp = g4 * 4 + j
                    P.mm(ps[b][:, j * 128:(j + 1) * 128], qt[:, hp, :], keysT[:, hp, :], True, True,
                         [r_qt, r_c], [psr[b]])
                    yield
                P.cp(sc[:, g4 * 4:g4 * 4 + 4, :], ps[b][:, :].rearrange('p (a b) -> p a b', a=4), [psr[b]], [r_sc],
                     eng='act')
                yield
            yield from top16_gen(P, [(sc[:, hp, :], r_sc, sc2[:, hp, :], r_sc2s[hp], sv[:, hp, :], si[:, hp, :], r_svs[hp])
                            for hp in range(16)])
            P.cp(sif[:, :, :], si[:, :, :], r_svs, [r_sif], eng='act')
            yield
            P.tt(comb[:, :, :, :], svr[:, :, 0, :].unsqueeze(3).broadcast_to([128, 8, 16, 16]),
                 svr[:, :, 1, :].unsqueeze(2).broadcast_to([128, 8, 16, 16]), ALU.add, r_svs + r_sc2s, [r_comb])
            yield
            yield from top16_gen(P, [(comb[:, h, :, :].rearrange('p i j -> p (i j)'), r_comb,
                             comb2[:, h, :, :].rearrange('p i j -> p (i j)'), r_sc2s[h], cv[:, h, :], ci[:, h, :],
                             r_cvs[h]) for h in range(8)])
            yield
            P.cp(cqf[:, 1, :, :], ci[:, :, :], r_cvs, [r_cq], eng='act')
            yield
            P.ts(cqf[:, 0, :, :], cqf[:, 1, :, :], 1.0 / 16.0, -0.46875, ALU.mult, ALU.add, [r_cq], [r_cq])
            yield
            P.cp(cq[:, 0, :, :], cqf[:, 0, :, :], [r_cq], [r_cq], eng='dve')
            yield
            P.cp(cqf[:, 0, :, :], cq[:, 0, :, :], [r_cq], [r_cq], eng='dve')
            yield
            P.stt(cqf[:, 1, :, :], cqf[:, 0, :, :], -16.0, cqf[:, 1, :, :], ALU.mult, ALU.add, [r_cq], [r_cq])
            yield
            r_eq = r_sc2s
            for p2 in range(2):
                P.tt(eq[:, :, :, :], iota16[:, :].unsqueeze(1).unsqueeze(1).broadcast_to([128, 8, 16, 16]),
                     cqf[:, p2, :, :].unsqueeze(3).broadcast_to([128, 8, 16, 16]), ALU.is_equal,
                     [r_c, r_cq] + r_cvs, r_eq)
                yield
                P.tt(eq[:, :, :, :], eq[:, :, :, :], sifr[:, :, p2, :].unsqueeze(2).broadcast_to([128, 8, 16, 16]),
                     ALU.mult, r_eq + [r_sif], r_eq)
                yield
                P.op('dve', lambda e, p2=p2: e.tensor_reduce(
                    out=idx[:, p2, :], in_=eq[:, :, :, :].rearrange('p h k i -> p (h k) i'), axis=AX.X, op=ALU.add),
                    r_eq, [r_idx])
                yield
            P.stt(eidf[:, :], idx[:, 0, :], 128.0, idx[:, 1, :], ALU.mult, ALU.add, [r_idx], [r_eidf])
            yield
            P.cp(eid[s][:, :], eidf[:, :], [r_eidf], [r_eid[s]], eng='dve')
            yield
            gt_ = gate[s]
            P.tt(gt_[:, :, :], cv[:, :, :], cv[:, :, 0:1].broadcast_to([128, 8, 16]), ALU.subtract, r_cvs, [r_gate[s]])
            yield
            P.act(gt_[:, :, :], gt_[:, :, :], AF.Exp, [r_gate[s]], [r_gate[s]])
            yield
            P.op('dve', lambda e: e.tensor_reduce(out=gsum[:, :], in_=gt_[:, :, :], axis=AX.X, op=ALU.add),
                 [r_gate[s]], [r_gsum])
            yield
            P.op('dve', lambda e: e.reciprocal(gsum[:, :], gsum[:, :]), [r_gsum], [r_gsum])
            yield
            P.tt(gt_[:, :, :], gt_[:, :, :], gsum[:, :].unsqueeze(2).broadcast_to([128, 8, 16]), ALU.mult,
                 [r_gate[s], r_gsum], [r_gate[s]])
            yield

        slot_ctr = [0]
        dg_ctr = [0]
        NPB = 3
        prod = [sbuf(nc, st, 'prod%d' % i, [128, 1024], BF16) for i in range(NPB)]
        r_prod = P.Rs('prod', NPB)
        NJA = 3
        junkAs = [sbuf(nc, st, 'junkA%d' % i, [128, 1024], BF16) for i in range(NJA)]
        r_junkA = P.Rs('junkA', NJA)
        ja_ctr = [0]
        pb_ctr = [0]
        r_actc = [P.Rs('actc%d_' % i, 128) for i in range(2)]

        def experts(tt, rgen=None, pull=2):
            s = tt % 2
            yb = [4 + 2 * s, 5 + 2 * s]
            gflat = gate[s][:, :, :].rearrange('p h k -> p (h k)')
            for g in range(128 // NG):
                c0 = g * NG
                slots = []
                for j in range(NG):
                    hk = c0 + j
                    sl = slot_ctr[0] % NS
                    slot_ctr[0] += 1
                    slots.append(sl)
                    if not DBG['nogather']:
                        P.op('pool', lambda e, sl=sl, hk=hk, s=s: e.indirect_dma_start(
                            out=uvg[sl][:, :], out_offset=None, in_=uvb_d,
                            in_offset=bass.IndirectOffsetOnAxis(ap=eid[s][:, hk:hk + 1], axis=0)),
                            [r_eid[s]], [r_uv[sl]], dma=True)
                for j in range(NG):
                    hk = c0 + j
                    sl = slots[j]
                    if SPLIT_DOTS and (j % 2 == 1):
                        pb = pb_ctr[0] % NPB
                        pb_ctr[0] += 1
                        P.tt(prod[pb][:, :], xb[s][:, :], uvg[sl][:, 0:1024], ALU.mult, [r_xb[s], r_uv[sl]],
                             [r_prod[pb]])
                        ja = ja_ctr[0] % NJA
                        ja_ctr[0] += 1
                        P.op('act', lambda e, pb=pb, hk=hk, s=s, ja=ja: e.activation(
                            junkAs[ja][:, :], prod[pb][:, :], AF.Copy, accum_out=actv[s][:, hk:hk + 1]),
                            [r_prod[pb]], [r_actc[s][hk], r_junkA[ja]])
                    else:
                        jk = jk_ctr[0] % NJ
                        jk_ctr[0] += 1
                        P.op('dve', lambda e, sl=sl, hk=hk, s=s, jk=jk: e.scalar_tensor_tensor(
                            junks[jk][:, :], xb[s][:, :], 1.0, uvg[sl][:, 0:1024], ALU.mult, ALU.mult,
                            accum_out=actv[s][:, hk:hk + 1]),
                            [r_xb[s], r_uv[sl]], [r_actc[s][hk], r_junk[jk]])
                    if rgen is not None:
                        for _ in range(pull):
                            next(rgen, None)
                a_ = actv[s][:, c0:c0 + NG]
                t_ = gtmp[s][:, c0:c0 + NG]
                ra, rt, rc = r_actc[s][c0:c0 + NG], r_gt[s][g], r_coef[s][g]
                P.act(t_, a_, AF.Square, ra, [rt])
                P.ts(t_, t_, 0.044715, 1.0, ALU.mult, ALU.add, [rt], [rt])
                P.tt(t_, t_, a_, ALU.mult, [rt] + ra, [rt])
                P.act(t_, t_, AF.Sigmoid, [rt], [rt], scale=GELU_K)
                P.tt(t_, t_, a_, ALU.mult, [rt] + ra, [rt])
                P.tt(coef[s][:, c0:c0 + NG], t_, gflat[:, c0:c0 + NG], ALU.mult, [rt, r_gate[s]], [rc])
                for j in range(NG):
                    hk = c0 + j
                    sl = slots[j]
                    d = dg_ctr[0] % NDG
                    dg_ctr[0] += 1
                    P.act(dg[d][:, :], ident[:, :], AF.Copy, [r_c, rc], [r_dg[d]], scale=coef[s][:, hk:hk + 1])
                    for half in range(2):
                        P.mm(ps[yb[half]][:, :], dg[d][:, :], uvg[sl][:, 1024 + half * 512:1024 + (half + 1) * 512],
                             hk == 0, hk == 127, [r_dg[d], r_uv[sl]], [psr[yb[half]]])
            for half in range(2):
                P.stt(zt[s][:, half * 512:(half + 1) * 512], xt[s][:, half * 512:(half + 1) * 512], ALPHA,
                      ps[yb[half]][:, :], ALU.mult, ALU.add, [r_xt[s], psr[yb[half]]], [r_z[s]])
            ln_store(C, L, zt[s], r_z[s], x_out[tt * 128:(tt + 1) * 128, :])

        for _ in routing(0):
            pass
        for tt in range(NT):
            rgen = routing(tt + 1) if tt + 1 < NT else None
            experts(tt, rgen)
            if rgen is not None:
                for _ in rgen:
                    pass
        P.flush()


def peer_host_layout(inp, layer):
    wq = inp['peer_w_q'][layer]
    wq_l = np.ascontiguousarray(wq.reshape(8, 128, 16, 128).transpose(2, 1, 0, 3))
    keys = inp['peer_sub_keys'][layer].reshape(16, 128, 128)
    keysT = np.ascontiguousarray(keys.transpose(2, 0, 1))
    return {'wq': wq_l, 'keysT': keysT}


CH = 64
NCH = T // CH
RWKV_GN_EPS = 64e-5
DECAY_K = -math.exp(-0.5)


def phase_rwkv(C, x_in, x_out, W, S, stages=('A', 'A2', 'B', 'C'), nch_dbg=None):
    nc, P = C.nc, C.P
    ps, psr = C.ps, C.psr
    pi = 0

    def nb():
        nonlocal pi
        b = pi
        pi = (pi + 1) % 8
        return b

    QT = 512
    NQ = T // QT
    if 'A' in stages:
      with ExitStack() as st:
          ident = sbuf(nc, st, 'ident', [128, 128])
          xTh = sbuf(nc, st, 'xTh', [128, 8, T + 1])
          xl = [sbuf(nc, st, 'xl%d' % i, [128, 1024]) for i in range(2)]
          xm = [sbuf(nc, st, 'xm%d' % i, [128, 8, QT], BF16) for i in range(2)]
          wbig = [sbuf(nc, st, 'wbig%d' % i, [128, 8192]) for i in range(1)]
          wbb = [sbuf(nc, st, 'wbb%d' % i, [128, 8192], BF16) for i in range(2)]
          w2f = sbuf(nc, st, 'w2f', [128, 1024])
          r_wf, r_w2f = P.R('awf'), P.R('aw2f')
          stg = [sbuf(nc, st, 'stg%d' % i, [128, 512]) for i in range(4)]
          t1 = sbuf(nc, st, 't1', [128, QT], BF16)
          w2s = sbuf(nc, st, 'w2s', [128, 1024], BF16)
          cst = sbuf(nc, st, 'cstA', [128, 20, 8])
          r_c, r_xT = P.R('ac'), P.R('axT')
          r_xl = P.Rs('axl', 2)
          r_xm = P.Rs('axm', 2)
          r_wb = P.Rs('awb', 2)
          r_stg = P.Rs('astg', 4)
          r_t1, r_w2 = P.R('at1'), P.R('aw2')
          r_d = {k: P.R('ad_' + k) for k in ('r', 'k', 'lw', 'a', 'v', 'g')}
          P.dma(ident[:, :], W['ident'], [], [r_c])
          P.dma(cst[:, 0:6, :], W['rw_mix'], [], [r_c])
          P.dma(cst[:, 12:14, :], W['rw_w0a0'], [], [r_c])
          P.ts(cst[:, 6:12, :], cst[:, 0:6, :], -1.0, 1.0, ALU.mult, ALU.add, [r_c], [r_c])
          P.op('dve', lambda e: e.memset(xTh[:, :, 0:1], 0.0), [], [r_xT])
          for tt in range(T // 128):
              s = tt % 2
              P.dma(xl[s][:, :], x_in[tt * 128:(tt + 1) * 128, :], [], [r_xl[s]])
              for half in range(2):
                  b = nb()
                  for j in range(4):
                      k = half * 4 + j
                      P.tr(ps[b][:, j * 128:(j + 1) * 128], xl[s][:, k * 128:(k + 1) * 128], ident[:, :],
                           [r_xl[s], r_c], [psr[b]])
                  P.cp(xTh[:, half * 4:half * 4 + 4, 1 + tt * 128:1 + (tt + 1) * 128],
                       ps[b][:, :].rearrange('p (a b) -> p a b', a=4), [psr[b]], [r_xT],
                       eng=('act' if half else 'dve'))
          sti = 0
          xmi = 0
          for mi, m in enumerate(('r', 'k', 'lw', 'a', 'v', 'g')):
              wf = wbig[0]
              wb = wbb[mi % 2]
              rwb = r_wb[mi % 2]
              if m in ('r', 'k'):
                  P.dma(wf[:, :].rearrange('p (c k j) -> p c k j', c=8, k=8), W['rw_w' + m], [], [r_wf])
                  P.cp(wb[:, 0:4096], wf[:, 0:4096], [r_wf], [rwb], eng='act')
                  P.cp(wb[:, 4096:8192], wf[:, 4096:8192], [r_wf], [rwb], eng='dve')
              elif m in ('lw', 'a'):
                  P.dma(wf[:, 0:512].rearrange('p (k j) -> p k j', k=8), W['rw_%s1' % ('w' if m == 'lw' else 'a')],
                        [], [r_wf])
                  P.cp(wb[:, 0:512], wf[:, 0:512], [r_wf], [rwb], eng='act')
                  P.dma(w2f[0:64, :], W['rw_%s2' % ('w' if m == 'lw' else 'a')], [], [r_w2f])
                  P.cp(w2s[0:64, :], w2f[0:64, :], [r_w2f], [r_w2], eng='act')
              elif m == 'v':
                  P.dma(wf[:, :].rearrange('p (k j) -> p k j', k=8), W['rw_wv'], [], [r_wf])
                  P.cp(wb[:, 0:4096], wf[:, 0:4096], [r_wf], [rwb], eng='act')
                  P.cp(wb[:, 4096:8192], wf[:, 4096:8192], [r_wf], [rwb], eng='dve')
              else:
                  P.dma(wf[:, 0:1024].rearrange('p (k j) -> p k j', k=8), W['rw_g1'], [], [r_wf])
                  P.cp(wb[:, 0:1024], wf[:, 0:1024], [r_wf], [rwb], eng='act')
                  P.dma(w2f[:, :], W['rw_g2'], [], [r_w2f])
                  P.cp(w2s[:, :], w2f[:, :], [r_w2f], [r_w2], eng='act')
              for tq in range(NQ):
                  t0 = tq * QT
                  xs = xm[xmi % 2]
                  rxs = r_xm[xmi % 2]
                  xmi += 1
                  for k in range(8):
                      P.act(xs[:, k, :], xTh[:, k, t0:t0 + QT], AF.Copy, [r_xT, r_c], [rxs],
                            scale=cst[:, mi, k:k + 1])
                      P.stt(xs[:, k, :], xTh[:, k, t0 + 1:t0 + 1 + QT], cst[:, 6 + mi, k:k + 1], xs[:, k, :],
                            ALU.mult, ALU.add, [r_xT, r_c, rxs], [rxs])
                  if m in ('r', 'k'):
                      wv4 = wb[:, :].rearrange('p (c k j) -> p c k j', c=8, k=8)
                      for c in range(8):
                          b = nb()
                          for k in range(8):
                              P.mm(ps[b][:, :], wv4[:, c, k, :], xs[:, k, :], k == 0, k == 7, [rwb, rxs], [psr[b]])
                          sg = sti % 4
                          sti += 1
                          P.cp(stg[sg][:, :], ps[b][:, :], [psr[b]], [r_stg[sg]], eng=('act' if c % 2 else 'dve'))
                          P.dma(S[m][c * 128:(c + 1) * 128, t0:t0 + QT], stg[sg][:, :], [r_stg[sg]], [r_d[m]], q='pool')
                  elif m in ('lw', 'a'):
                      w1v = wb[:, 0:512].rearrange('p (k j) -> p k j', k=8)
                      b = nb()
                      for k in range(8):
                          P.mm(ps[b][0:64, :], w1v[:, k, :], xs[:, k, :], k == 0, k == 7, [rwb, rxs], [psr[b]])
                      if m == 'lw':
                          P.act(t1[0:64, :], ps[b][0:64, :], AF.Tanh, [psr[b]], [r_t1])
                      else:
                          P.cp(t1[0:64, :], ps[b][0:64, :], [psr[b]], [r_t1], eng='act')
                      for c in range(8):
                          b = nb()
                          P.mm(ps[b][:, :], w2s[0:64, c * 128:(c + 1) * 128], t1[0:64, :], True, True,
                               [r_w2, r_t1], [psr[b]])
                          sg = sti % 4
                          sti += 1
                          P.act(stg[sg][:, :], ps[b][:, :], AF.Sigmoid, [psr[b], r_c], [r_stg[sg]],
                                bias=cst[:, 12 + (0 if m == 'lw' else 1), c:c + 1])
                          if m == 'lw':
                              P.ts(stg[sg][:, :], stg[sg][:, :], DECAY_K, None, ALU.mult, None, [r_stg[sg]], [r_stg[sg]])
                          P.dma(S[m][c * 128:(c + 1) * 128, t0:t0 + QT], stg[sg][:, :], [r_stg[sg]], [r_d[m]], q='pool')
                  elif m == 'v':
                      wv3 = wb[:, :].rearrange('p (k j) -> p k j', k=8)
                      for tt in range(4):
                          for half in range(2):
                              b = nb()
                              for k in range(8):
                                  P.mm(ps[b][:, :], xs[:, k, tt * 128:(tt + 1) * 128],
                                       wv3[:, k, half * 512:(half + 1) * 512], k == 0, k == 7, [rwb, rxs], [psr[b]])
                              sg = sti % 4
                              sti += 1
                              P.cp(stg[sg][:, :], ps[b][:, :], [psr[b]], [r_stg[sg]], eng=('act' if half else 'dve'))
                              P.dma(S['v'][t0 + tt * 128:t0 + (tt + 1) * 128, half * 512:(half + 1) * 512],
                                    stg[sg][:, :], [r_stg[sg]], [r_d[m]], q='pool')
                  else:
                      g1v = wb[:, 0:1024].rearrange('p (k j) -> p k j', k=8)
                      b = nb()
                      for k in range(8):
                          P.mm(ps[b][:, :], g1v[:, k, :], xs[:, k, :], k == 0, k == 7, [rwb, rxs], [psr[b]])
                      P.act(t1[:, :], ps[b][:, :], AF.Sigmoid, [psr[b]], [r_t1])
                      for tt in range(4):
                          for half in range(2):
                              b = nb()
                              P.mm(ps[b][:, :], t1[:, tt * 128:(tt + 1) * 128], w2s[:, half * 512:(half + 1) * 512],
                                   True, True, [r_t1, r_w2], [psr[b]])
                              sg = sti % 4
                              sti += 1
                              P.cp(stg[sg][:, :], ps[b][:, :], [psr[b]], [r_stg[sg]], eng=('act' if half else 'dve'))
                              P.dma(S['g'][t0 + tt * 128:t0 + (tt + 1) * 128, half * 512:(half + 1) * 512],
                                    stg[sg][:, :], [r_stg[sg]], [r_d[m]], q='pool')
          P.flush()

    if 'A2' in stages:
      with ExitStack() as st:
          cst = sbuf(nc, st, 'cstB', [128, 5, 8])
          bones = sbuf(nc, st, 'bones', [128, 128])
          hsel = sbuf(nc, st, 'hsel', [128, 2])
          rmask = sbuf(nc, st, 'rmask', [128, QT])
          names = ('r', 'k', 'lw', 'a')
          tin = {n: [sbuf(nc, st, 'in_%s%d' % (n, i), [128, QT]) for i in range(2)] for n in names}
          r_in = {n: P.Rs('bin_' + n, 2) for n in names}
          tmps = {n: sbuf(nc, st, 'tmp_' + n, [128, QT]) for n in ('kkr', 'sq', 'nrm', 'kp', 'cl', 'eP', 'eN', 'ePm')}
          r_t = {n: P.R('bt_' + n) for n in tmps}
          onames = ('rb', 'kb', 'bb', 'ab')
          tout = {n: [sbuf(nc, st, 'o_%s%d' % (n, i), [128, QT], BF16) for i in range(2)] for n in onames}
          r_out = {n: P.Rs('bo_' + n, 2) for n in onames}
          rkr = sbuf(nc, st, 'rkr', [128, QT])
          r_rkr = P.R('rkr')
          rho_st = sbuf(nc, st, 'rho_st', [128, 4, 16])
          r_rho = P.R('rho')
          gcs = [sbuf(nc, st, 'gcs%d' % i, [128, 8]) for i in range(2)]
          r_gcs = P.Rs('gcs', 2)
          r_c = P.R('bc')
          r_dd = P.R('bdram')
          P.dma(cst[:, 0:3, :], W['rw_kkr'], [], [r_c])
          P.dma(bones[:, :], W['bones'], [], [r_c])
          P.dma(hsel[:, :], W['hsel'], [], [r_c])
          P.dma(rmask[:, :], W['rmask'], [], [r_c])
          P.ts(cst[:, 3, :], cst[:, 1, :], -1.0, 1.0, ALU.mult, ALU.add, [r_c], [r_c])
          it = 0
          for tq in range(NQ):
              t0 = tq * QT
              for c in range(8):
                  s = it % 2
                  it += 1
                  for n in names:
                      P.dma(tin[n][s][:, :], S[n][c * 128:(c + 1) * 128, t0:t0 + QT], [], [r_in[n][s]])
                  r_, k_, lw_, a_ = [tin[n][s] for n in names]
                  rr_, rk_, rlw_, ra_ = [r_in[n][s] for n in names]
                  T_ = tmps
                  P.act(T_['kkr'][:, :], k_[:, :], AF.Copy, [rk_, r_c], [r_t['kkr']], scale=cst[:, 0, c:c + 1])
                  P.act(T_['sq'][:, :], T_['kkr'][:, :], AF.Square, [r_t['kkr']], [r_t['sq']])
                  b = nb()
                  P.mm(ps[b][:, :], bones[:, :], T_['sq'][:, :], True, True, [r_c, r_t['sq']], [psr[b]])
                  P.act(T_['nrm'][:, :], ps[b][:, :], AF.Sqrt, [psr[b]], [r_t['nrm']])
                  P.ts(T_['nrm'][:, :], T_['nrm'][:, :], 1e-12, None, ALU.max, None, [r_t['nrm']], [r_t['nrm']])
                  P.op('dve', lambda e: e.reciprocal(T_['nrm'][:, :], T_['nrm'][:, :]), [r_t['nrm']], [r_t['nrm']])
                  P.tt(T_['kkr'][:, :], T_['kkr'][:, :], T_['nrm'][:, :], ALU.mult, [r_t['kkr'], r_t['nrm']],
                       [r_t['kkr']])
                  P.ts(T_['kp'][:, :], a_[:, :], cst[:, 1, c:c + 1], cst[:, 3, c:c + 1], ALU.mult, ALU.add,
                       [ra_, r_c], [r_t['kp']])
                  P.tt(T_['kp'][:, :], T_['kp'][:, :], k_[:, :], ALU.mult, [r_t['kp'], rk_], [r_t['kp']])
                  P.op('dve', lambda e, lw_=lw_: e.tensor_tensor_scan(T_['cl'][:, :], rmask[:, :], lw_[:, :], 0.0,
                                                                       ALU.mult, ALU.add),
                       [r_c, rlw_], [r_t['cl']])
                  P.act(T_['eP'][:, :], T_['cl'][:, :], AF.Exp, [r_t['cl']], [r_t['eP']])
                  P.act(T_['eN'][:, :], T_['cl'][:, :], AF.Exp, [r_t['cl']], [r_t['eN']], scale=-1.0)
                  P.tt(T_['cl'][:, :], T_['cl'][:, :], lw_[:, :], ALU.subtract, [r_t['cl'], rlw_, r_t['eP'], r_t['eN']],
                       [r_t['cl']])
                  P.act(T_['ePm'][:, :], T_['cl'][:, :], AF.Exp, [r_t['cl']], [r_t['ePm']])
                  o = {n: tout[n][s] for n in onames}
                  ro = {n: r_out[n][s] for n in onames}
                  P.tt(o['rb'][:, :], r_[:, :], T_['eP'][:, :], ALU.mult, [rr_, r_t['eP']], [ro['rb']])
                  P.tt(o['kb'][:, :], T_['kp'][:, :], T_['eN'][:, :], ALU.mult, [r_t['kp'], r_t['eN']], [ro['kb']])
                  P.tt(T_['sq'][:, :], T_['kkr'][:, :], a_[:, :], ALU.mult, [r_t['kkr'], ra_], [r_t['sq']])
                  P.tt(o['bb'][:, :], T_['sq'][:, :], T_['eN'][:, :], ALU.mult, [r_t['sq'], r_t['eN']], [ro['bb']])
                  P.stt(o['ab'][:, :], T_['kkr'][:, :], -1.0, T_['ePm'][:, :], ALU.mult, ALU.mult,
                        [r_t['kkr'], r_t['ePm']], [ro['ab']])
                  for n in onames:
                      P.dma(S[n][c * 128:(c + 1) * 128, t0:t0 + QT], o[n][:, :], [ro[n]], [r_dd], q='pool')
                  gs = it % 2
                  P.cp(gcs[gs][:, :], T_['eP'][:, :].rearrange('p (c t) -> p c t', t=CH)[:, :, CH - 1], [r_t['eP']],
                       [r_gcs[gs]], eng='act')
                  P.dma(S['gC'][c * 128:(c + 1) * 128, tq * 8:(tq + 1) * 8], gcs[gs][:, :], [r_gcs[gs]], [r_dd], q='pool')
                  P.stt(rkr[:, :], r_[:, :], cst[:, 2, c:c + 1], T_['kp'][:, :], ALU.mult, ALU.mult,
                        [rr_, r_c, r_t['kp']], [r_rkr])
                  b = nb()
                  for tt in range(4):
                      P.mm(ps[b][:, tt * 2:tt * 2 + 2], rkr[:, tt * 128:(tt + 1) * 128], hsel[:, :], True, True,
                           [r_rkr, r_c], [psr[b]])
                  P.cp(rho_st[:, :, 2 * c:2 * c + 2], ps[b][:, 0:8].rearrange('p (a b) -> p a b', b=2), [psr[b]],
                       [r_rho], eng='act')
              P.dma(S['rho'][t0:t0 + QT, :].rearrange('(a p) h -> p a h', p=128), rho_st[:, :, :], [r_rho], [r_dd], q='pool')
          P.flush()

    if 'B' in stages:
      with ExitStack() as st:
          HH = 8
          mSU = sbuf(nc, st, 'mSU', [64, HH, 64])
          mSL = sbuf(nc, st, 'mSL', [64, HH, 64])
          mUI = sbuf(nc, st, 'mUI', [64, HH, 64])
          mI = sbuf(nc, st, 'mI', [64, HH, 64])
          id64 = sbuf(nc, st, 'id64', [64, 64])
          gC = sbuf(nc, st, 'gC', [64, 16, NCH])
          lnxg = sbuf(nc, st, 'lnxg', [64, 1024])
          lnxb = sbuf(nc, st, 'lnxb', [64, 1024])
          r_c = P.R('sc')
          P.dma(mSU[:, :, :], W['mSU'], [], [r_c])
          P.dma(mSL[:, :, :], W['mSL'], [], [r_c])
          P.dma(mUI[:, :, :], W['mUI'], [], [r_c])
          P.dma(mI[:, :, :], W['mI'], [], [r_c])
          P.dma(id64[:, :], W['ident'][0:64, 0:64], [], [r_c])
          P.dma(gC[:, :, :], S['gC'].rearrange('(h k) c -> k h c', k=64), [], [r_c])
          P.dma(lnxg[:, :], W['rw_lnxg'][0:64, :], [], [r_c])
          P.dma(lnxb[:, :], W['rw_lnxb'][0:64, :], [], [r_c])
          lnames = ('ab', 'bb', 'kb', 'rb')
          lin = {n: [sbuf(nc, st, 'l_%s%d' % (n, i), [64, 16, CH], BF16) for i in range(2)] for n in lnames}
          r_lin = {n: P.Rs('sl_' + n, 2) for n in lnames}
          vt = [sbuf(nc, st, 'vt%d' % i, [64, 1024]) for i in range(2)]
          gt = [sbuf(nc, st, 'gt%d' % i, [64, 1024]) for i in range(2)]
          rh = [sbuf(nc, st, 'rh%d' % i, [64, 16]) for i in range(2)]
          r_vt, r_gt, r_rh = P.Rs('svt', 2), P.Rs('sgt', 2), P.Rs('srh', 2)
          mats = ('N', 'NT', 'Aak', 'Abr', 'Akr', 'bbT', 'kbT', 'P0', 'P1', 'Ma', 'Mb', 'MTa', 'MTb')
          mt = [{n: sbuf(nc, st, 'm_%s%d' % (n, hf), [64, HH, 64], BF16) for n in mats} for hf in range(2)]
          r_mt = [{n: P.R('sm_%s%d' % (n, hf)) for n in mats} for hf in range(2)]
          Y = sbuf(nc, st, 'Y', [64, 16, 64], BF16)
          XT = sbuf(nc, st, 'XT', [64, 16, 64], BF16)
          r_Y, r_XT = P.Rs('sY', 2), P.Rs('sXT', 2)
          ST = [sbuf(nc, st, 'ST%d' % i, [64, 16, 64]) for i in range(2)]
          r_ST = [P.Rs('sST%d_' % i, 2) for i in range(2)]
          O = [sbuf(nc, st, 'O%d' % i, [64, 16, 64]) for i in range(2)]
          r_O = P.Rs('sO', 2)
          Oc = sbuf(nc, st, 'Oc', [64, 16, 64])
          r_Oc = P.R('sOc')
          stat = sbuf(nc, st, 'stat', [64, 4, 16])
          r_stat = P.R('sstat')
          r_yd = P.R('syd')
          P.op('dve', lambda e: e.memset(ST[0][:, :, :], 0.0), [], r_ST[0])
          STb = [sbuf(nc, st, 'STb%d' % i, [64, 16, 64], BF16) for i in range(2)]
          r_STb = [P.Rs('sSTb%d_' % i, 2) for i in range(2)]
          P.op('dve', lambda e: e.memset(STb[0][:, :, :], 0.0), [], r_STb[0])
          id64b = sbuf(nc, st, 'id64b', [64, 64], BF16)
          P.cp(id64b[:, :], id64[:, :], [r_c], [r_c])
          vtb = [sbuf(nc, st, 'vtb%d' % i, [64, 1024], BF16) for i in range(2)]
          r_vtb = P.Rs('svtb', 2)
          stmp = [sbuf(nc, st, 'stmp%d' % i, [64, HH, 64]) for i in range(2)]
          r_stmp = P.Rs('sstmp', 2)
          gcx = [sbuf(nc, st, 'gcx%d' % i, [64, 16, 64]) for i in range(2)]
          r_gcx = P.Rs('sgcx', 2)

          def f2(t):
              return t.rearrange('p a b -> p (a b)') if len(t.shape) == 3 else t[:, :, :].rearrange('p a b -> p (a b)')

          def prod(hf, lhs, rhs_, rl, rr2, evac, dst, rdst, extra_r=()):
              b = nb()
              for j in range(HH):
                  h = hf * HH + j
                  P.mm(ps[b][0:64, j * 64:(j + 1) * 64], lhs(h), rhs_(h), True, True, rl + rr2, [psr[b]])
              pv = ps[b][0:64, :]
              evac(pv, psr[b])

          for c in range(NCH if nch_dbg is None else nch_dbg):
              s = c % 2
              t0 = c * CH
              for n in lnames:
                  P.dma(lin[n][s][:, :, :], S[n][:, t0:t0 + CH].rearrange('(h k) t -> k h t', k=64), [], [r_lin[n][s]])
              P.dma(vt[s][:, :], S['v'][t0:t0 + CH, :], [], [r_vt[s]])
              P.dma(gt[s][:, :], S['g'][t0:t0 + CH, :], [], [r_gt[s]])
              P.dma(rh[s][:, :], S['rho'][t0:t0 + CH, :], [], [r_rh[s]])
              P.cp(gcx[s][:, :, :], gC[:, :, c:c + 1].broadcast_to([64, 16, 64]), [r_c], [r_gcx[s]], eng='act')
              ab, bb, kb, rb = [lin[n][s] for n in lnames]
              rab, rbb, rkb, rrb = [[r_lin[n][s]] for n in lnames]
              vt3 = vt[s][:, :].rearrange('p (h v) -> p h v', h=16)
              P.cp(vtb[s][:, :], vt[s][:, :], [r_vt[s]], [r_vtb[s]], eng='act')
              vb3 = vtb[s][:, :].rearrange('p (h v) -> p h v', h=16)
              STbo, STbn = STb[c % 2], STb[(c + 1) % 2]
              rSTbo, rSTbn = r_STb[c % 2], r_STb[(c + 1) % 2]
              STo, STn = ST[c % 2], ST[(c + 1) % 2]
              rSTo, rSTn = r_ST[c % 2], r_ST[(c + 1) % 2]
              for hf in range(2):
                  M_, R_ = mt[hf], r_mt[hf]

                  def ev_mask(dst, rd, mask, eng='dve'):
                      return lambda pv, pr: P.tt(f2(dst), pv, f2(mask), ALU.mult, [pr, r_c], [rd], eng=eng)

                  def ev_copy(dst, rd, eng='act'):
                      return lambda pv, pr: P.cp(f2(dst), pv, [pr], [rd], eng=eng)

                  prod(hf, lambda h: bb[:, h, :], lambda h: ab[:, h, :], rbb, rab, ev_mask(M_['N'], R_['N'], mSU), None, None)
                  prod(hf, lambda h: ab[:, h, :], lambda h: bb[:, h, :], rab, rbb, ev_mask(M_['NT'], R_['NT'], mSL), None, None)
                  prod(hf, lambda h: kb[:, h, :], lambda h: ab[:, h, :], rkb, rab, ev_mask(M_['Aak'], R_['Aak'], mSU), None, None)
                  prod(hf, lambda h: bb[:, h, :], lambda h: rb[:, h, :], rbb, rrb, ev_mask(M_['Abr'], R_['Abr'], mUI), None, None)
                  prod(hf, lambda h: kb[:, h, :], lambda h: rb[:, h, :], rkb, rrb, ev_mask(M_['Akr'], R_['Akr'], mUI), None, None)
                  prod(hf, lambda h: bb[:, h, :], lambda h: id64b[:, :], rbb, [r_c], ev_copy(M_['bbT'], R_['bbT']), None, None)
                  prod(hf, lambda h: kb[:, h, :], lambda h: id64b[:, :], rkb, [r_c], ev_copy(M_['kbT'], R_['kbT']), None, None)
                  P.tt(f2(M_['P0']), f2(M_['N']), f2(mI), ALU.add, [R_['N'], r_c], [R_['P0']])
                  Mc, MTc, rMc, rMTc = M_['N'], M_['NT'], R_['N'], R_['NT']
                  Pc, Pn, rPc, rPn = M_['P0'], M_['P1'], R_['P0'], R_['P1']
                  bufs = [('Ma', 'MTa'), ('Mb', 'MTb')]
                  for lvl in range(5):
                      mn, mtn = bufs[lvl % 2]
                      prod(hf, (lambda h, Mc=Mc: Mc[:, h % HH, :]), (lambda h, MTc=MTc: MTc[:, h % HH, :]), [rMc], [rMTc],
                           ev_copy(M_[mtn], R_[mtn], eng='act'), None, None)
                      if lvl < 4:
                          prod(hf, (lambda h, MTc=MTc: MTc[:, h % HH, :]), (lambda h, Mc=Mc: Mc[:, h % HH, :]), [rMTc], [rMc],
                               ev_copy(M_[mn], R_[mn], eng='dve'), None, None)
                      prod(hf, (lambda h, m2t=M_[mtn]: m2t[:, h % HH, :]), (lambda h, Pc=Pc: Pc[:, h % HH, :]), [R_[mtn]], [rPc],
                           (lambda pv, pr, Pc=Pc, Pn=Pn, rPc=rPc, rPn=rPn:
                            P.tt(f2(Pn), pv, f2(Pc), ALU.add, [pr, rPc], [rPn])), None, None)
                      Mc, MTc, rMc, rMTc = M_[mn], M_[mtn], R_[mn], R_[mtn]
                      Pc, Pn, rPc, rPn = Pn, Pc, rPn, rPc
                  M_['Pf'], R_['Pf'] = Pc, rPc
              for hf in range(2):
                  M_, R_ = mt[hf], r_mt[hf]
                  b = nb()
                  for j in range(HH):
                      h = hf * HH + j
                      P.mm(ps[b][0:64, j * 64:(j + 1) * 64], ab[:, h, :], STbo[:, h, :], True, False,
                           rab + [rSTbo[hf]], [psr[b]])
                      P.mm(ps[b][0:64, j * 64:(j + 1) * 64], M_['Aak'][:, j, :], vb3[:, h, :], False, True,
                           [R_['Aak'], r_vtb[s]], [psr[b]])
                  P.cp(f2(Y[:, hf * HH:(hf + 1) * HH, :]), ps[b][0:64, :], [psr[b]],
                       [r_Y[hf]], eng=('act' if hf else 'dve'))
              for hf in range(2):
                  M_, R_ = mt[hf], r_mt[hf]
                  b = nb()
                  for j in range(HH):
                      h = hf * HH + j
                      P.mm(ps[b][0:64, j * 64:(j + 1) * 64], M_['Pf'][:, j, :], Y[:, h, :], True, True,
                           [R_['Pf'], r_Y[hf]], [psr[b]])
                  P.cp(f2(XT[:, hf * HH:(hf + 1) * HH, :]), ps[b][0:64, :], [psr[b]],
                       [r_XT[hf]], eng=('act' if hf else 'dve'))
              for hf in range(2):
                  M_, R_ = mt[hf], r_mt[hf]
                  b = nb()
                  for j in range(HH):
                      h = hf * HH + j
                      o_ = ps[b][0:64, j * 64:(j + 1) * 64]
                      P.mm(o_, M_['bbT'][:, j, :], XT[:, h, :], True, False, [R_['bbT'], r_XT[hf]], [psr[b]])
                      P.mm(o_, M_['kbT'][:, j, :], vb3[:, h, :], False, True, [R_['kbT'], r_vtb[s]], [psr[b]])
                  P.tt(f2(stmp[hf]), ps[b][0:64, :], f2(STo[:, hf * HH:(hf + 1) * HH, :]), ALU.add,
                       [psr[b], rSTo[hf]], [r_stmp[hf]])
                  P.tt(f2(STn[:, hf * HH:(hf + 1) * HH, :]), f2(stmp[hf]), f2(gcx[s][:, hf * HH:(hf + 1) * HH, :]),
                       ALU.mult, [r_stmp[hf], r_gcx[s]], [rSTn[hf]])
                  P.cp(f2(STbn[:, hf * HH:(hf + 1) * HH, :]), f2(STn[:, hf * HH:(hf + 1) * HH, :]), [rSTn[hf]],
                       [rSTbn[hf]], eng='act')
                  b = nb()
                  for j in range(HH):
                      h = hf * HH + j
                      o_ = ps[b][0:64, j * 64:(j + 1) * 64]
                      P.mm(o_, rb[:, h, :], STbo[:, h, :], True, False, rrb + [rSTbo[hf]], [psr[b]])
                      P.mm(o_, M_['Abr'][:, j, :], XT[:, h, :], False, False, [R_['Abr'], r_XT[hf]], [psr[b]])
                      P.mm(o_, M_['Akr'][:, j, :], vb3[:, h, :], False, True, [R_['Akr'], r_vtb[s]], [psr[b]])
                  P.cp(f2(O[s][:, hf * HH:(hf + 1) * HH, :]), ps[b][0:64, :], [psr[b]],
                       [r_O[s]], eng='act')
              Os = O[s]
              P.op('dve', lambda e, Os=Os: e.tensor_reduce(out=stat[:, 0, :], in_=Os[:, :, :], axis=AX.X, op=ALU.add),
                   [r_O[s]], [r_stat])
              P.ts(stat[:, 0, :], stat[:, 0, :], 1.0 / 64.0, None, ALU.mult, None, [r_stat], [r_stat])
              P.tt(Oc[:, :, :], Os[:, :, :], stat[:, 0, :].unsqueeze(2).broadcast_to([64, 16, 64]), ALU.subtract,
                   [r_O[s], r_stat], [r_Oc])
              P.act(Os[:, :, :], Oc[:, :, :], AF.Square, [r_Oc], [r_O[s]])
              P.op('dve', lambda e, Os=Os: e.tensor_reduce(out=stat[:, 1, :], in_=Os[:, :, :], axis=AX.X, op=ALU.add),
                   [r_O[s]], [r_stat])
              P.ts(stat[:, 1, :], stat[:, 1, :], 1.0 / 64.0, RWKV_GN_EPS, ALU.mult, ALU.add, [r_stat], [r_stat])
              P.act(stat[:, 1, :], stat[:, 1, :], AF.Sqrt, [r_stat], [r_stat])
              P.op('dve', lambda e: e.reciprocal(stat[:, 1, :], stat[:, 1, :]), [r_stat], [r_stat])
              P.tt(Oc[:, :, :], Oc[:, :, :], stat[:, 1, :].unsqueeze(2).broadcast_to([64, 16, 64]), ALU.mult,
                   [r_Oc, r_stat], [r_Oc])
              Of = Oc[:, :, :].rearrange('p h v -> p (h v)')
              P.tt(Of, Of, lnxg[:, :], ALU.mult, [r_Oc, r_c], [r_Oc])
              P.tt(Of, Of, lnxb[:, :], ALU.add, [r_Oc, r_c], [r_Oc])
              P.tt(Os[:, :, :], vt3, rh[s][:, :].unsqueeze(2).broadcast_to([64, 16, 64]), ALU.mult,
                   [r_vt[s], r_rh[s], r_O[s]], [r_O[s]])
              P.tt(Of, Of, Os[:, :, :].rearrange('p h v -> p (h v)'), ALU.add, [r_Oc, r_O[s]], [r_Oc])
              P.tt(Of, Of, gt[s][:, :], ALU.mult, [r_Oc, r_gt[s]], [r_Oc])
              P.dma(S['y'][t0:t0 + CH, :], Of, [r_Oc], [r_yd], q='pool')
          P.flush()

    if 'C' in stages:
      with ExitStack() as st:
          ident = sbuf(nc, st, 'ident', [128, 128])
          wo = sbuf(nc, st, 'wo', [128, 8, 1024])
          yl = [sbuf(nc, st, 'yl%d' % i, [128, 1024]) for i in range(2)]
          xl = [sbuf(nc, st, 'xl%d' % i, [128, 1024]) for i in range(2)]
          yT = [sbuf(nc, st, 'yT%d' % i, [128, 8, 128]) for i in range(2)]
          L = ln_setup(C, st, W['ln_g'], W['ln_b'], 'rw')
          L['q'] = 'pool'
          r_c = P.R('cc')
          r_yl, r_xl, r_yT = P.Rs('cyl', 2), P.Rs('cxl', 2), P.Rs('cyT', 2)
          P.dma(ident[:, :], W['ident'], [], [r_c])
          P.dma(wo[:, :, :], W['rw_wo'], [], [r_c])
          for tt in range(T // 128 if nch_dbg is None else nch_dbg):
              s = tt % 2
              P.dma(yl[s][:, :], S['y'][tt * 128:(tt + 1) * 128, :], [], [r_yl[s]])
              P.dma(xl[s][:, :], x_in[tt * 128:(tt + 1) * 128, :], [], [r_xl[s]])
              for half in range(2):
                  b = nb()
                  for j in range(4):
                      k = half * 4 + j
                      P.tr(ps[b][:, j * 128:(j + 1) * 128], yl[s][:, k * 128:(k + 1) * 128], ident[:, :],
                           [r_yl[s], r_c], [psr[b]])
                  P.cp(yT[s][:, half * 4:half * 4 + 4, :].rearrange('p a b -> p (a b)'), ps[b][:, :],
                       [psr[b]], [r_yT[s]], eng=('act' if half else 'dve'))
              bs = [nb(), nb()]
              for half in range(2):
                  for k in range(8):
                      P.mm(ps[bs[half]][:, :], yT[s][:, k, :], wo[:, k, half * 512:(half + 1) * 512], k == 0, k == 7,
                           [r_yT[s], r_c], [psr[bs[half]]])
              for half in range(2):
                  P.stt(yl[s][:, half * 512:(half + 1) * 512], xl[s][:, half * 512:(half + 1) * 512], ALPHA,
                        ps[bs[half]][:, :], ALU.mult, ALU.add, [r_xl[s], psr[bs[half]], r_yT[s]], [r_yl[s]])
              ln_store(C, L, yl[s], r_yl[s], x_out[tt * 128:(tt + 1) * 128, :])
          P.flush()


def rwkv_scratch(nc, tag, kind='Internal'):
    S = {}
    for n in ('r', 'k', 'lw', 'a'):
        S[n] = nc.dram_tensor('rs_%s_%s' % (tag, n), [1024, T], F32, kind=kind).ap()
    for n in ('rb', 'kb', 'bb', 'ab'):
        S[n] = nc.dram_tensor('rs_%s_%s' % (tag, n), [1024, T], BF16, kind=kind).ap()
    for n in ('v', 'g', 'y'):
        S[n] = nc.dram_tensor('rs_%s_%s' % (tag, n), [T, 1024], F32, kind=kind).ap()
    S['gC'] = nc.dram_tensor('rs_%s_gC' % tag, [1024, NCH], F32, kind=kind).ap()
    S['rho'] = nc.dram_tensor('rs_%s_rho' % tag, [T, 16], F32, kind=kind).ap()
    return S


def rwkv_host_layout(inp):
    f = np.float32
    g = lambda n: inp[n][0]

    def lhsT_chunks(w, ncol_chunks):
        return np.ascontiguousarray(w.reshape(8, 128, ncol_chunks, 128).transpose(1, 2, 0, 3))

    def rows_pk(w):
        return np.ascontiguousarray(w.reshape(8, 128, -1).transpose(1, 0, 2))

    H = {}
    H['rw_wr'] = lhsT_chunks(g('rw_w_r'), 8)
    H['rw_wk'] = lhsT_chunks(g('rw_w_k'), 8)
    H['rw_wv'] = rows_pk(g('rw_w_v'))
    H['rw_wo'] = rows_pk(g('rw_w_o'))
    H['rw_w1'] = rows_pk(g('rw_w1'))
    H['rw_a1'] = rows_pk(g('rw_a1'))
    H['rw_g1'] = rows_pk(g('rw_g1'))
    H['rw_w2'] = np.ascontiguousarray(g('rw_w2'))
    H['rw_a2'] = np.ascontiguousarray(g('rw_a2'))
    H['rw_g2'] = np.ascontiguousarray(g('rw_g2'))
    mix = g('rw_mix')[[0, 2, 1, 4, 3, 5]]
    H['rw_mix'] = np.ascontiguousarray(mix.reshape(6, 8, 128).transpose(2, 0, 1))
    H['rw_w0a0'] = np.ascontiguousarray(np.stack([g('rw_w0'), g('rw_a0')]).reshape(2, 8, 128).transpose(2, 0, 1))
    H['rw_kkr'] = np.ascontiguousarray(
        np.stack([g('rw_k_k'), g('rw_k_a'), g('rw_r_k').reshape(-1)]).reshape(3, 8, 128).transpose(2, 0, 1))
    H['rw_lnxg'] = bc128(g('rw_lnx_g'))
    H['rw_lnxb'] = bc128(g('rw_lnx_b'))
    bones = np.zeros((128, 128), f)
    bones[:64, :64] = 1
    bones[64:, 64:] = 1
    H['bones'] = bones
    hsel = np.zeros((128, 2), f)
    hsel[:64, 0] = 1
    hsel[64:, 1] = 1
    H['hsel'] = hsel
    rm = np.ones((128, 512), f)
    rm[:, ::CH] = 0
    H['rmask'] = rm
    s_ = np.arange(64)[:, None]
    t_ = np.arange(64)[None, :]
    rep = lambda m: np.ascontiguousarray(np.broadcast_to(m.astype(f)[:, None, :], (64, 8, 64)))
    H['mSU'] = rep(s_ < t_)
    H['mSL'] = rep(s_ > t_)
    H['mUI'] = rep(s_ <= t_)
    H['mI'] = rep(s_ == t_)
    return H


FUSED = True
PHASES = ('rg', 'peer0', 'rwkv', 'peer1')


def host_arrays(inp, ph):
    ident = np.eye(128, dtype=np.float32)
    if ph == 'rg':
        H = rg_host_layout(inp)
        H['ln_g'] = bc128(inp['ln_g'][0, 0])
        H['ln_b'] = bc128(inp['ln_b'][0, 0])
    elif ph == 'rwkv':
        H = rwkv_host_layout(inp)
        H['ln_g'] = bc128(inp['ln_g'][1, 0])
        H['ln_b'] = bc128(inp['ln_b'][1, 0])
    else:
        layer = int(ph[-1])
        H = peer_host_layout(inp, layer)
        H['iota16'] = bc128(np.arange(16, dtype=np.float32))
        H['ln_g'] = bc128(inp['ln_g'][layer, 1])
        H['ln_b'] = bc128(inp['ln_b'][layer, 1])
        H['uv'] = np.ascontiguousarray(np.concatenate([inp['peer_u'][layer], inp['peer_v'][layer]], axis=1))
    H['ident'] = ident
    return {k: np.ascontiguousarray(v, dtype=np.float32) for k, v in H.items()}


def build_program(phs, HA):
    nc = bass.Bass("TRN2", target_bir_lowering=False)
    xin = nc.dram_tensor('xin', [T, D], F32, kind='ExternalInput').ap()
    xout = nc.dram_tensor('xout', [T, D], F32, kind='ExternalOutput').ap()
    Ws = {}
    for ph in phs:
        Ws[ph] = {k: nc.dram_tensor('%s_%s' % (ph, k), list(v.shape), F32, kind='ExternalInput').ap()
                  for k, v in HA[ph].items()}
    acts = [xin]
    for i in range(len(phs) - 1):
        acts.append(nc.dram_tensor('act%d' % i, [T, D], F32, kind='Internal').ap())
    acts.append(xout)
    with ExitStack() as st:
        C = make_ctx(nc, st)
        for i, ph in enumerate(phs):
            if ph == 'rg':
                phase_rg(C, acts[i], acts[i + 1], Ws[ph])
            elif ph == 'rwkv':
                S = rwkv_scratch(nc, ph)
                phase_rwkv(C, acts[i], acts[i + 1], Ws[ph], S)
            else:
                qT_d = nc.dram_tensor('qT_' + ph, [16, 128, T], F32, kind='Internal').ap()
                uvb_d = nc.dram_tensor('uvb_' + ph, [16384, 2048], BF16, kind='Internal').ap()
                phase_peer(C, acts[i], acts[i + 1], Ws[ph], qT_d, uvb_d, ph)
    return nc


def run_launch(phs, HA, x_cores):
    nc = build_program(phs, HA)
    base = {}
    for ph in phs:
        for k, v in HA[ph].items():
            base['%s_%s' % (ph, k)] = v
    in_maps = []
    for b in range(NCORES):
        m = dict(base)
        m['xin'] = np.ascontiguousarray(x_cores[b], dtype=np.float32)
        in_maps.append(m)
    res = run_bass_kernel_spmd(nc, in_maps, core_ids=list(range(NCORES)))
    return [np.asarray(res.results[b]['xout']) for b in range(NCORES)]


def kernel(**inputs):
    inp = {k: np.asarray(v) for k, v in inputs.items()}
    x = [inp['x'][b] for b in range(NCORES)]
    groups = [PHASES] if FUSED else [(p,) for p in PHASES]
    for phs in groups:
        HA = {ph: host_arrays(inp, ph) for ph in phs}
        x = run_launch(phs, HA, x)
    return np.stack(x, axis=0).astype(np.float32)
```

```python
import math
from contextlib import ExitStack
import numpy as np
import concourse.bass as bass
import concourse.mybir as mybir
from concourse.bass_utils import run_bass_kernel_spmd

F32 = mybir.dt.float32
BF16 = mybir.dt.bfloat16
U32 = mybir.dt.uint32
I32 = mybir.dt.int32
AF = mybir.ActivationFunctionType
ALU = mybir.AluOpType
AX = mybir.AxisListType

D = 1024
T = 2048
NCORES = 8
RG_W = 1408
RG_H = 16
RG_B = 88
NWC = 11
ALPHA = 4.0 ** 0.25
LN_EPS = 1e-5
GELU_K = 2.0 * math.sqrt(2.0 / math.pi)

ENG = {'pe': 'tensor', 'dve': 'vector', 'act': 'scalar', 'pool': 'gpsimd', 'sp': 'sync'}
NPOOL = 32


class Res:
    __slots__ = ('name', 'w', 'rs')

    def __init__(self, name):
        self.name = name
        self.w = None
        self.rs = []


class Op:
    __slots__ = ('eng', 'fn', 'deps', 'signal', 'is_dma', 'sem', 'val', 'semname')

    def __init__(self, eng, fn, is_dma):
        self.eng = eng
        self.fn = fn
        self.deps = []
        self.signal = False
        self.is_dma = is_dma
        self.sem = None
        self.val = 0
        self.semname = None


class Prog:
    def __init__(self, nc, st):
        self.nc = nc
        self.ops = []
        self.res = []
        self.sems = {e: st.enter_context(nc.semaphore('s_' + e)) for e in ENG}
        self.cnt = {e: 0 for e in ENG}
        self.dsem = {q: [st.enter_context(nc.semaphore('d_%s_%d' % (q, i))) for i in range(NPOOL)]
                     for q in ('sp', 'pool', 'act')}
        self.duse = {q: [0] * NPOOL for q in self.dsem}
        self.dlast = {q: [None] * NPOOL for q in self.dsem}
        self.drr = {q: 0 for q in self.dsem}
        self.waited = {e: {} for e in ENG}
        self.nblk = 0

    def R(self, name):
        r = Res(name)
        self.res.append(r)
        return r

    def Rs(self, name, n):
        return [self.R('%s%d' % (name, i)) for i in range(n)]

    def op(self, eng, fn, reads=(), writes=(), dma=False):
        o = Op(eng, fn, dma)
        deps = []
        for r in reads:
            if r.w is not None:
                deps.append(r.w)
        for w in writes:
            if w.w is not None:
                deps.append(w.w)
            deps.extend(w.rs)
        if dma:
            q = eng
            j = self.drr[q]
            self.drr[q] = (j + 1) % NPOOL
            if self.dlast[q][j] is not None:
                deps.append(self.dlast[q][j])
            self.duse[q][j] += 1
            o.sem = self.dsem[q][j]
            o.semname = 'd_%s_%d' % (q, j)
            o.val = 16 * self.duse[q][j]
            o.signal = True
            self.dlast[q][j] = o
        seen = set()
        for d in deps:
            if d is o or id(d) in seen:
                continue
            seen.add(id(d))
            if (not d.is_dma) and (not dma) and d.eng == 'pe' and eng == 'pe':
                continue
            d.signal = True
            o.deps.append(d)
        for r in reads:
            r.rs.append(o)
        for w in writes:
            w.w = o
            w.rs = []
        self.ops.append(o)
        return o

    def dma(self, out, in_, reads=(), writes=(), q='sp'):
        return self.op(q, lambda e: e.dma_start(out=out, in_=in_), reads, writes, dma=True)

    def mm(self, out, lhsT, rhs, start, stop, reads=(), writes=()):
        return self.op('pe', lambda e: e.matmul(out, lhsT, rhs, start=start, stop=stop), reads, writes)

    def tr(self, out, in_, ident, reads=(), writes=()):
        return self.op('pe', lambda e: e.transpose(out, in_, ident), reads, writes)

    def act(self, out, in_, func, reads=(), writes=(), bias=None, scale=None):
        kw = {}
        if bias is not None:
            kw['bias'] = bias
        if scale is not None:
            kw['scale'] = scale
        return self.op('act', lambda e: e.activation(out, in_, func, **kw), reads, writes)

    def tt(self, out, in0, in1, op, reads=(), writes=(), eng='dve'):
        return self.op(eng, lambda e: e.tensor_tensor(out, in0, in1, op), reads, writes)

    def ts(self, out, in0, s1, s2, op0, op1=None, reads=(), writes=(), eng='dve'):
        if op1 is None:
            return self.op(eng, lambda e: e.tensor_scalar(out, in0, s1, None, op0), reads, writes)
        return self.op(eng, lambda e: e.tensor_scalar(out, in0, s1, s2, op0, op1), reads, writes)

    def stt(self, out, in0, scalar, in1, op0, op1, reads=(), writes=()):
        return self.op('dve', lambda e: e.scalar_tensor_tensor(out, in0, scalar, in1, op0, op1), reads, writes)

    def cp(self, out, in_, reads=(), writes=(), eng='dve'):
        if eng == 'act':
            return self.op('act', lambda e: e.copy(out, in_), reads, writes)
        return self.op(eng, lambda e: e.tensor_copy(out, in_), reads, writes)

    def flush(self):
        last = {}
        for o in self.ops:
            last[o.eng if not o.is_dma else ('dma', o.semname)] = o
        tails = []
        for k, o in last.items():
            o.signal = True
            tails.append(o)
        used = sorted({o.eng for o in self.ops})
        for e in used:
            b = Op(e, None, False)
            b.deps = [t for t in tails]
            self.ops.append(b)
        for o in self.ops:
            if o.is_dma or o.fn is None:
                continue
            if o.signal:
                self.cnt[o.eng] += 1
                o.sem = self.sems[o.eng]
                o.semname = 's_' + o.eng
                o.val = self.cnt[o.eng]
        nc = self.nc
        ops = self.ops
        with nc.Block() as block:
            for ename, attr in ENG.items():
                eops = [o for o in ops if o.eng == ename]
                if not eops:
                    continue

                def body(engine, eops=eops, ename=ename):
                    wt = self.waited[ename]
                    for o in eops:
                        for d in o.deps:
                            if wt.get(d.semname, 0) < d.val:
                                engine.wait_ge(d.sem, d.val)
                                wt[d.semname] = d.val
                        if o.fn is not None:
                            inst = o.fn(engine)
                            if o.is_dma:
                                inst.then_inc(o.sem, 16)
                            elif o.signal:
                                inst.then_inc(o.sem, 1)

                getattr(block, attr)(body)
        self.ops = []
        for r in self.res:
            r.w = None
            r.rs = []
        self.dlast = {q: [None] * NPOOL for q in self.dsem}
        self.nblk += 1


class Ctx:
    pass


def make_ctx(nc, st):
    C = Ctx()
    C.nc = nc
    C.P = Prog(nc, st)
    C.ps = [st.enter_context(nc.psum_tensor('ps%d' % i, [128, 512], F32)) for i in range(8)]
    C.psr = C.P.Rs('psr', 8)
    return C


_uid = [0]


def sbuf(nc, st, name, shape, dt=F32):
    _uid[0] += 1
    return st.enter_context(nc.sbuf_tensor('sb%d_%s' % (_uid[0], name), shape, dt))


def ln_store(C, L, z, zr, out_ap, pp=128):
    P = C.P
    st6, mv, sd = L['st6'], L['mv'], L['sd']
    r6, rmv, rsd = L['r6'], L['rmv'], L['rsd']
    for h in range(2):
        P.op('dve', lambda e, h=h: e.bn_stats(st6[:pp, h, :], z[:pp, h * 512:(h + 1) * 512]), [zr], [r6])
    P.op('dve', lambda e: e.bn_aggr(mv[:pp, :], st6[:pp, :, :].rearrange('p a b -> p (a b)')), [r6], [rmv])
    P.ts(sd[:pp, 0:1], mv[:pp, 1:2], LN_EPS, None, ALU.add, None, [rmv], [rsd])
    P.act(sd[:pp, 0:1], sd[:pp, 0:1], AF.Sqrt, [rsd], [rsd])
    P.op('dve', lambda e: e.reciprocal(sd[:pp, 0:1], sd[:pp, 0:1]), [rsd], [rsd])
    P.ts(z[:pp, :], z[:pp, :], mv[:pp, 0:1], sd[:pp, 0:1], ALU.subtract, ALU.mult, [zr, rmv, rsd], [zr])
    P.tt(z[:pp, :], z[:pp, :], L['g'][:pp, :], ALU.mult, [zr, L['rg']], [zr])
    P.tt(z[:pp, :], z[:pp, :], L['b'][:pp, :], ALU.add, [zr, L['rg']], [zr])
    P.dma(out_ap, z[:pp, :], [zr], [L['rout']], q=L.get('q', 'sp'))


def ln_setup(C, st, g_d, b_d, tag):
    nc, P = C.nc, C.P
    L = {}
    L['g'] = sbuf(nc, st, 'lng' + tag, [128, 1024])
    L['b'] = sbuf(nc, st, 'lnb' + tag, [128, 1024])
    L['st6'] = sbuf(nc, st, 'lnst' + tag, [128, 2, 6])
    L['mv'] = sbuf(nc, st, 'lnmv' + tag, [128, 2])
    L['sd'] = sbuf(nc, st, 'lnsd' + tag, [128, 2])
    L['rg'] = P.R('lnrg')
    L['r6'] = P.R('lnr6')
    L['rmv'] = P.R('lnrmv')
    L['rsd'] = P.R('lnrsd')
    L['rout'] = P.R('lnrout')
    P.dma(L['g'][:, :], g_d, [], [L['rg']])
    P.dma(L['b'][:, :], b_d, [], [L['rg']])
    return L


def gelu_tanh(P, x, xr, t, tr_, eng2='dve'):
    P.act(t, x, AF.Square, [xr], [tr_])
    P.ts(t, t, 0.044715, 1.0, ALU.mult, ALU.add, [tr_], [tr_])
    P.tt(t, t, x, ALU.mult, [tr_, xr], [tr_])
    P.act(t, t, AF.Sigmoid, [tr_], [tr_], scale=GELU_K)
    P.tt(x, x, t, ALU.mult, [xr, tr_], [xr])


def phase_rg(C, x_in, x_out, W):
    nc, P = C.nc, C.P
    SEG = 256
    NSEG = T // SEG
    with ExitStack() as st:
        ident = sbuf(nc, st, 'ident', [128, 128])
        wout = sbuf(nc, st, 'wout', [128, NWC, 1024], BF16)
        wstg = sbuf(nc, st, 'wstg', [128, 1024])
        r_wstg = P.R('wstg')
        waB = sbuf(nc, st, 'waB', [128, NWC, 3, 128])
        wxB = sbuf(nc, st, 'wxB', [128, NWC, 3, 128])
        cst = sbuf(nc, st, 'cst', [128, 9, NWC])
        wbuf = [sbuf(nc, st, 'wbuf%d' % i, [128, 8, 128]) for i in range(3)]
        wbb = [sbuf(nc, st, 'wbb%d' % i, [128, 8, 128], BF16) for i in range(3)]
        r_wbb = P.Rs('wbb', 3)
        xtm = [sbuf(nc, st, 'xtm%d' % i, [128, 1024]) for i in range(2)]
        xT = sbuf(nc, st, 'xT', [128, 8, SEG], BF16)
        hgb = sbuf(nc, st, 'hgb', [128, NWC, SEG], BF16)
        r_hgb = P.R('hgb')
        hg = sbuf(nc, st, 'hg', [128, NWC, SEG])
        xcp = sbuf(nc, st, 'xcp', [128, NWC, SEG + 3])
        xc = sbuf(nc, st, 'xc', [128, NWC, SEG])
        rr = sbuf(nc, st, 'rr', [128, NWC, SEG])
        ii = sbuf(nc, st, 'ii', [128, NWC, SEG])
        aa = sbuf(nc, st, 'aa', [128, NWC, SEG])
        hs = sbuf(nc, st, 'hs', [128, NWC, SEG])
        zt = [sbuf(nc, st, 'zt%d' % i, [128, 1024]) for i in range(2)]
        L = ln_setup(C, st, W['ln_g'], W['ln_b'], 'rg')
        L['q'] = 'pool'
        r_const = P.R('const')
        r_wbuf = P.Rs('wbuf', 3)
        r_xtm = P.Rs('xtm', 2)
        r_xT, r_hg, r_xcp, r_xc, r_rr, r_ii, r_aa, r_hs = [P.R(n) for n in
                                                          ('xT', 'hg', 'xcp', 'xc', 'rr', 'ii', 'aa', 'hs')]
        r_zt = P.Rs('zt', 2)
        ps, psr = C.ps, C.psr

        P.dma(ident[:, :], W['ident'], [], [r_const])
        for c in range(NWC):
            P.dma(wstg[:, :], W['rg_w_out'][:, c, :], [], [r_wstg])
            P.cp(wout[:, c, :], wstg[:, :], [r_wstg], [r_const], eng=('act' if c % 2 else 'dve'))
        P.dma(waB[:, :, :, :], W['rg_wa'], [], [r_const])
        P.dma(wxB[:, :, :, :], W['rg_wx'], [], [r_const])
        P.dma(cst[:, :, :], W['rg_cst'], [], [r_const])
        P.act(cst[:, 7, :], cst[:, 7, :], AF.Exp, [r_const], [r_const], scale=-1.0)
        P.act(cst[:, 7, :], cst[:, 7, :], AF.Ln, [r_const], [r_const], bias=1.0)
        P.ts(cst[:, 8, :], cst[:, 7, :], -16.0, None, ALU.mult, None, [r_const], [r_const])
        P.ts(cst[:, 7, :], cst[:, 7, :], -8.0, None, ALU.mult, None, [r_const], [r_const])
        P.op('dve', lambda e: e.memset(xcp[:, :, 0:3], 0.0), [], [r_xcp])
        P.op('dve', lambda e: e.memset(hs[:, :, SEG - 1:SEG], 0.0), [], [r_hs])

        pi = 0

        def nb():
            nonlocal pi
            b = pi
            pi = (pi + 1) % 8
            return b

        for s in range(NSEG):
            t0 = s * SEG
            for tt in range(2):
                P.dma(xtm[tt][:, :], x_in[t0 + tt * 128:t0 + (tt + 1) * 128, :], [], [r_xtm[tt]])
                for half in range(2):
                    b = nb()
                    for j in range(4):
                        k = half * 4 + j
                        P.tr(ps[b][:, j * 128:(j + 1) * 128], xtm[tt][:, k * 128:(k + 1) * 128], ident[:, :],
                             [r_xtm[tt], r_const], [psr[b]])
                    P.cp(xT[:, half * 4:half * 4 + 4, tt * 128:(tt + 1) * 128],
                         ps[b][:, :].rearrange('p (a b) -> p a b', a=4), [psr[b]], [r_xT], eng='act')
            for oc in range(2 * NWC):
                wb = oc % 3
                P.dma(wbuf[wb][:, :, :], W['rg_w_in'][oc], [], [r_wbuf[wb]])
                P.cp(wbb[wb][:, :, :], wbuf[wb][:, :, :], [r_wbuf[wb]], [r_wbb[wb]], eng=('act' if oc % 2 else 'dve'))
                b = nb()
                for k in range(8):
                    P.mm(ps[b][:, 0:SEG], wbb[wb][:, k, :], xT[:, k, :], k == 0, k == 7,
                         [r_wbb[wb], r_xT], [psr[b]])
                if oc < NWC:
                    P.cp(hg[:, oc, :], ps[b][:, 0:SEG], [psr[b]], [r_hg], eng='act')
                else:
                    P.cp(xcp[:, oc - NWC, 3:3 + SEG], ps[b][:, 0:SEG], [psr[b]], [r_xcp], eng='act')
            for c in range(NWC):
                P.ts(xc[:, c, :], xcp[:, c, 0:SEG], cst[:, 0, c:c + 1], cst[:, 4, c:c + 1], ALU.mult, ALU.add,
                     [r_xcp, r_const], [r_xc])
                for j in range(1, 4):
                    P.stt(xc[:, c, :], xcp[:, c, j:j + SEG], cst[:, j, c:c + 1], xc[:, c, :], ALU.mult, ALU.add,
                          [r_xcp, r_const, r_xc], [r_xc])
            P.cp(xcp[:, :, 0:3], xcp[:, :, SEG:SEG + 3], [r_xcp, r_xc], [r_xcp], eng='act')
            for (wB, bias_i, dst, rdst) in ((waB, 5, rr, r_rr), (wxB, 6, ii, r_ii)):
                for co in range(NWC):
                    b = nb()
                    cis = [ci for ci in (co - 1, co, co + 1) if 0 <= ci < NWC]
                    for n, ci in enumerate(cis):
                        P.mm(ps[b][:, 0:SEG], wB[:, co, ci - co + 1, :], xc[:, ci, :], n == 0, n == len(cis) - 1,
                             [r_const, r_xc], [psr[b]])
                    P.act(dst[:, co, :], ps[b][:, 0:SEG], AF.Sigmoid, [psr[b], r_const], [rdst],
                          bias=cst[:, bias_i, co:co + 1])
            for c in range(NWC):
                P.act(aa[:, c, :], rr[:, c, :], AF.Exp, [r_rr, r_const], [r_aa], scale=cst[:, 7, c:c + 1])
                P.act(rr[:, c, :], rr[:, c, :], AF.Exp, [r_rr, r_const], [r_rr], scale=cst[:, 8, c:c + 1])
            P.ts(rr[:, :, :], rr[:, :, :], -1.0, 1.0, ALU.mult, ALU.add, [r_rr], [r_rr])
            P.ts(rr[:, :, :], rr[:, :, :], 0.0, None, ALU.max, None, [r_rr], [r_rr])
            P.act(rr[:, :, :], rr[:, :, :], AF.Sqrt, [r_rr], [r_rr])
            P.tt(ii[:, :, :], ii[:, :, :], xc[:, :, :], ALU.mult, [r_ii, r_xc], [r_ii])
            P.tt(ii[:, :, :], ii[:, :, :], rr[:, :, :], ALU.mult, [r_ii, r_rr], [r_ii])
            P.cp(xc[:, :, 0:1], hs[:, :, SEG - 1:SEG], [r_hs, r_ii], [r_xc], eng='dve')
            for c in range(NWC):
                P.op('dve', lambda e, c=c: e.tensor_tensor_scan(hs[:, c, :], aa[:, c, :], ii[:, c, :],
                                                                 xc[:, c, 0:1], ALU.mult, ALU.add),
                     [r_aa, r_ii, r_xc], [r_hs])
            gelu_tanh(P, hg[:, :, :], r_hg, rr[:, :, :], r_rr)
            P.tt(hgb[:, :, :], hg[:, :, :], hs[:, :, :], ALU.mult, [r_hg, r_hs], [r_hgb])
            for tt in range(2):
                bs = [nb(), nb()]
                for half in range(2):
                    for c in range(NWC):
                        P.mm(ps[bs[half]][:, :], hgb[:, c, tt * 128:(tt + 1) * 128],
                             wout[:, c, half * 512:(half + 1) * 512], c == 0, c == NWC - 1,
                             [r_hgb, r_const], [psr[bs[half]]])
                for half in range(2):
                    P.stt(zt[tt][:, half * 512:(half + 1) * 512], xtm[tt][:, half * 512:(half + 1) * 512], ALPHA,
                          ps[bs[half]][:, :], ALU.mult, ALU.add, [r_xtm[tt], psr[bs[half]]], [r_zt[tt]])
                ln_store(C, L, zt[tt], r_zt[tt], x_out[t0 + tt * 128:t0 + (tt + 1) * 128, :])
        P.flush()


def _vec_pc(v, nchunk):
    return np.ascontiguousarray(v.reshape(nchunk, 128).T)


def rg_host_layout(inp):
    f = np.float32
    w_in = inp['rg_w_in'][0]
    w_in_l = np.ascontiguousarray(w_in.reshape(8, 128, 22, 128).transpose(2, 1, 0, 3))
    w_out_l = np.ascontiguousarray(inp['rg_w_out'][0].reshape(NWC, 128, 1024).transpose(1, 0, 2))

    def band(w):
        full = np.zeros((RG_W + 256, RG_W + 256), f)
        for h in range(RG_H):
            full[128 + h * RG_B:128 + (h + 1) * RG_B, 128 + h * RG_B:128 + (h + 1) * RG_B] = w[h]
        out = np.zeros((128, NWC, 3, 128), f)
        for co in range(NWC):
            for kk in range(3):
                ci = co - 1 + kk
                out[:, co, kk, :] = full[128 + ci * 128:128 + (ci + 1) * 128, 128 + co * 128:128 + (co + 1) * 128]
        return out

    cst = np.zeros((128, 9, NWC), f)
    for j in range(4):
        cst[:, j, :] = _vec_pc(inp['rg_conv_w'][0, j], NWC)
    cst[:, 4, :] = _vec_pc(inp['rg_conv_b'][0], NWC)
    cst[:, 5, :] = _vec_pc(inp['rg_b_a'][0].reshape(-1), NWC)
    cst[:, 6, :] = _vec_pc(inp['rg_b_x'][0].reshape(-1), NWC)
    cst[:, 7, :] = _vec_pc(inp['rg_lambda'][0].reshape(-1), NWC)
    return {'rg_w_in': w_in_l, 'rg_w_out': w_out_l, 'rg_wa': band(inp['rg_w_a'][0]),
            'rg_wx': band(inp['rg_w_x'][0]), 'rg_cst': cst}


def bc128(v):
    return np.ascontiguousarray(np.broadcast_to(v.reshape(1, -1), (128, v.size))).astype(np.float32)


NEG = -1.0e30
SPLIT_DOTS = True
DBG = {'nogather': False, 'nodve': False, 'halfrow': False}


def top16_gen(P, items):
    for (vals, rv, scratch, rscr, outv, outi, rout) in items:
        P.op('dve', lambda e, outv=outv, vals=vals: e.max(out=outv[:, 0:8], in_=vals), [rv], [rout])
        yield
    for (vals, rv, scratch, rscr, outv, outi, rout) in items:
        P.op('dve', lambda e, outv=outv, outi=outi, vals=vals: e.max_index(out=outi[:, 0:8], in_max=outv[:, 0:8],
                                                                         in_values=vals), [rv, rout], [rout])
        yield
    for (vals, rv, scratch, rscr, outv, outi, rout) in items:
        P.op('dve', lambda e, outv=outv, vals=vals, scratch=scratch: e.match_replace(
            out=scratch, in_to_replace=outv[:, 0:8], in_values=vals, imm_value=NEG), [rv, rout], [rscr])
        yield
    for (vals, rv, scratch, rscr, outv, outi, rout) in items:
        P.op('dve', lambda e, outv=outv, scratch=scratch: e.max(out=outv[:, 8:16], in_=scratch), [rscr], [rout])
        yield
    for (vals, rv, scratch, rscr, outv, outi, rout) in items:
        P.op('dve', lambda e, outv=outv, outi=outi, scratch=scratch: e.max_index(
            out=outi[:, 8:16], in_max=outv[:, 8:16], in_values=scratch), [rscr, rout], [rout])
        yield


def top16_batch(P, items):
    for (vals, rv, scratch, rscr, outv, outi, rout) in items:
        P.op('dve', lambda e, outv=outv, vals=vals: e.max(out=outv[:, 0:8], in_=vals), [rv], [rout])
    for (vals, rv, scratch, rscr, outv, outi, rout) in items:
        P.op('dve', lambda e, outv=outv, outi=outi, vals=vals: e.max_index(out=outi[:, 0:8], in_max=outv[:, 0:8],
                                                                         in_values=vals), [rv, rout], [rout])
    for (vals, rv, scratch, rscr, outv, outi, rout) in items:
        P.op('dve', lambda e, outv=outv, vals=vals, scratch=scratch: e.match_replace(
            out=scratch, in_to_replace=outv[:, 0:8], in_values=vals, imm_value=NEG), [rv, rout], [rscr])
    for (vals, rv, scratch, rscr, outv, outi, rout) in items:
        P.op('dve', lambda e, outv=outv, scratch=scratch: e.max(out=outv[:, 8:16], in_=scratch), [rscr], [rout])
    for (vals, rv, scratch, rscr, outv, outi, rout) in items:
        P.op('dve', lambda e, outv=outv, outi=outi, scratch=scratch: e.max_index(
            out=outi[:, 8:16], in_max=outv[:, 8:16], in_values=scratch), [rscr, rout], [rout])


def phase_peer(C, x_in, x_out, W, qT_d, uvb_d, tag):
    nc, P = C.nc, C.P
    ps, psr = C.ps, C.psr
    NT = T // 128
    pi = 0

    def nb():
        nonlocal pi
        b = pi
        pi = (pi + 1) % 8
        return b

    with ExitStack() as st:
        ident = sbuf(nc, st, 'ident', [128, 128])
        xT = sbuf(nc, st, 'xT', [128, 8, T])
        xl = [sbuf(nc, st, 'xl%d' % i, [128, 1024]) for i in range(2)]
        wq = [sbuf(nc, st, 'wq%d' % i, [128, 8, 128]) for i in range(2)]
        qs = [sbuf(nc, st, 'qs%d' % i, [128, T]) for i in range(2)]
        r_c = P.R('qc')
        r_xT = P.R('qxT')
        r_xl = P.Rs('qxl', 2)
        r_wq = P.Rs('qwq', 2)
        r_qs = P.Rs('qqs', 2)
        r_qd = P.R('qTd')
        P.dma(ident[:, :], W['ident'], [], [r_c])
        for tt in range(NT):
            s = tt % 2
            P.dma(xl[s][:, :], x_in[tt * 128:(tt + 1) * 128, :], [], [r_xl[s]])
            for half in range(2):
                b = nb()
                for j in range(4):
                    k = half * 4 + j
                    P.tr(ps[b][:, j * 128:(j + 1) * 128], xl[s][:, k * 128:(k + 1) * 128], ident[:, :],
                         [r_xl[s], r_c], [psr[b]])
                P.cp(xT[:, half * 4:half * 4 + 4, tt * 128:(tt + 1) * 128],
                     ps[b][:, :].rearrange('p (a b) -> p a b', a=4), [psr[b]], [r_xT],
                     eng=('act' if half else 'dve'))
        cf = [sbuf(nc, st, 'cf%d' % i, [128, 4096]) for i in range(2)]
        cb = [sbuf(nc, st, 'cb%d' % i, [128, 4096], BF16) for i in range(2)]
        r_cf, r_cb, r_uvb = P.Rs('qcf', 2), P.Rs('qcb', 2), P.R('quvb')
        NCAST = 16384 // 256
        cast_i = [0]

        def cast_chunk():
            i = cast_i[0]
            if i >= NCAST:
                return
            cast_i[0] += 1
            s2 = i % 2
            P.dma(cf[s2][:, :].rearrange('p (r c) -> p r c', r=2),
                  W['uv'][i * 256:(i + 1) * 256, :].rearrange('(p r) c -> p r c', r=2), [], [r_cf[s2]], q='act')
            P.cp(cb[s2][:, :], cf[s2][:, :], [r_cf[s2]], [r_cb[s2]], eng='dve')
            P.dma(uvb_d[i * 256:(i + 1) * 256, :].rearrange('(p r) c -> p r c', r=2),
                  cb[s2][:, :].rearrange('p (r c) -> p r c', r=2), [r_cb[s2]], [r_uvb], q='act')

        for hp in range(16):
            s = hp % 2
            for _ in range(4):
                cast_chunk()
            P.dma(wq[s][:, :, :], W['wq'][hp], [], [r_wq[s]])
            for tc in range(4):
                b = nb()
                for k in range(8):
                    P.mm(ps[b][:, :], wq[s][:, k, :], xT[:, k, tc * 512:(tc + 1) * 512], k == 0, k == 7,
                         [r_wq[s], r_xT], [psr[b]])
                P.cp(qs[s][:, tc * 512:(tc + 1) * 512], ps[b][:, :], [psr[b]], [r_qs[s]],
                     eng=('act' if tc % 2 else 'dve'))
            P.dma(qT_d[hp], qs[s][:, :], [r_qs[s]], [r_qd])
        P.flush()

    with ExitStack() as st:
        NS = 24
        NG = 8
        ident = sbuf(nc, st, 'identT', [128, 128])
        keysT = sbuf(nc, st, 'keysT', [128, 16, 128])
        iota16 = sbuf(nc, st, 'iota16', [128, 16])
        xt = [sbuf(nc, st, 'xt%d' % i, [128, 1024]) for i in range(2)]
        xb = [sbuf(nc, st, 'xb%d' % i, [128, 1024], BF16) for i in range(2)]
        qt = sbuf(nc, st, 'qt', [128, 16, 128])
        sc = sbuf(nc, st, 'sc', [128, 16, 128])
        scr8 = sbuf(nc, st, 'scr8', [128, 2048])
        sc2 = scr8[:, :].rearrange('p (a b) -> p a b', a=16)
        comb2 = scr8[:, :].rearrange('p (h i j) -> p h i j', h=8, i=16)
        eq = comb2
        sv = sbuf(nc, st, 'sv', [128, 16, 16])
        si = sbuf(nc, st, 'si', [128, 16, 16], U32)
        sif = sbuf(nc, st, 'sif', [128, 16, 16])
        comb = sbuf(nc, st, 'comb', [128, 8, 16, 16])
        cv = sbuf(nc, st, 'cv', [128, 8, 16])
        ci = sbuf(nc, st, 'ci', [128, 8, 16], U32)
        cq = sbuf(nc, st, 'cq', [128, 2, 8, 16], I32)
        cqf = sbuf(nc, st, 'cqf', [128, 2, 8, 16])
        idx = sbuf(nc, st, 'idx', [128, 2, 128])
        eidf = sbuf(nc, st, 'eidf', [128, 128])
        eid = [sbuf(nc, st, 'eid%d' % i, [128, 128], I32) for i in range(2)]
        gate = [sbuf(nc, st, 'gate%d' % i, [128, 8, 16]) for i in range(2)]
        gsum = sbuf(nc, st, 'gsum', [128, 8])
        actv = [sbuf(nc, st, 'actv%d' % i, [128, 128]) for i in range(2)]
        gtmp = [sbuf(nc, st, 'gtmp%d' % i, [128, 128]) for i in range(2)]
        coef = [sbuf(nc, st, 'coef%d' % i, [128, 128]) for i in range(2)]
        NJ = 4
        junks = [sbuf(nc, st, 'junk%d' % i, [128, 1024], BF16) for i in range(NJ)]
        r_junk = P.Rs('junk', NJ)
        jk_ctr = [0]
        uvg = [sbuf(nc, st, 'uvg%d' % i, [128, 2048], BF16) for i in range(NS)]
        NDG = 8
        dg = [sbuf(nc, st, 'dg%d' % i, [128, 128], BF16) for i in range(NDG)]
        zt = [sbuf(nc, st, 'ztp%d' % i, [128, 1024]) for i in range(2)]
        L = ln_setup(C, st, W['ln_g'], W['ln_b'], 'pe' + tag)
        r_c = P.R('tc')
        r_xt = P.Rs('txt', 2)
        r_xb = P.Rs('txb', 2)
        r_qt = P.R('tqt')
        r_sc, r_scr, r_sif, r_comb, r_cq, r_idx, r_eidf, r_gsum = \
            [P.R(n) for n in ('sc', 'scr', 'sif', 'comb', 'cq', 'idx', 'eidf', 'gsum')]
        r_svs = P.Rs('svs', 16)
        r_sc2s = P.Rs('sc2s', 16)
        r_cvs = P.Rs('cvs', 8)
        r_eid = P.Rs('eid', 2)
        r_gate = P.Rs('gate', 2)
        r_act = [P.Rs('act%d_' % i, 128 // NG) for i in range(2)]
        r_gt = [P.Rs('gt%d_' % i, 128 // NG) for i in range(2)]
        r_coef = [P.Rs('coef%d_' % i, 128 // NG) for i in range(2)]
        r_uv = P.Rs('uv', NS)
        r_dg = P.Rs('dg', NDG)
        r_z = P.Rs('z', 2)
        r_qd = P.R('qTd2')
        P.dma(ident[:, :], W['ident'], [], [r_c])
        P.dma(keysT[:, :, :], W['keysT'], [], [r_c])
        P.dma(iota16[:, :], W['iota16'], [], [r_c])
        svr = sv[:, :, :].rearrange('p (h two) i -> p h two i', two=2)
        sifr = sif[:, :, :].rearrange('p (h two) i -> p h two i', two=2)
        pj = 0

        def nb4():
            nonlocal pj
            b = pj
            pj = (pj + 1) % 4
            return b

        def routing(tt):
            s = tt % 2
            P.dma(xt[s][:, :], x_in[tt * 128:(tt + 1) * 128, :], [], [r_xt[s]])
            yield
            P.dma(qt[:, :, :], qT_d[:, :, tt * 128:(tt + 1) * 128].rearrange('h p t -> p h t'), [r_qd], [r_qt])
            yield
            P.cp(xb[s][:, :], xt[s][:, :], [r_xt[s]], [r_xb[s]], eng='act')
            yield
            for g4 in range(4):
                b = nb4()
                yield
                for j in range(4):
                    hp = g4 * 4 + j
                    P.mm(ps[b][:, j * 128:(j + 1) * 128], qt[:, hp, :], keysT[:, hp, :], True, True,
                         [r_qt, r_c], [psr[b]])
                    yield
                P.cp(sc[:, g4 * 4:g4 * 4 + 4, :], ps[b][:, :].rearrange('p (a b) -> p a b', a=4), [psr[b]], [r_sc],
                     eng='act')
                yield
            yield from top16_gen(P, [(sc[:, hp, :], r_sc, sc2[:, hp, :], r_sc2s[hp], sv[:, hp, :], si[:, hp, :], r_svs[hp])
                            for hp in range(16)])
            P.cp(sif[:, :, :], si[:, :, :], r_svs, [r_sif], eng='act')
            yield
            P.tt(comb[:, :, :, :], svr[:, :, 0, :].unsqueeze(3).broadcast_to([128, 8, 16, 16]),
                 svr[:, :, 1, :].unsqueeze(2).broadcast_to([128, 8, 16, 16]), ALU.add, r_svs + r_sc2s, [r_comb])
            yield
            yield from top16_gen(P, [(comb[:, h, :, :].rearrange('p i j -> p (i j)'), r_comb,
                             comb2[:, h, :, :].rearrange('p i j -> p (i j)'), r_sc2s[h], cv[:, h, :], ci[:, h, :],
                             r_cvs[h]) for h in range(8)])
            yield
            P.cp(cqf[:, 1, :, :], ci[:, :, :], r_cvs, [r_cq], eng='act')
            yield
            P.ts(cqf[:, 0, :, :], cqf[:, 1, :, :], 1.0 / 16.0, -0.46875, ALU.mult, ALU.add, [r_cq], [r_cq])
            yield
            P.cp(cq[:, 0, :, :], cqf[:, 0, :, :], [r_cq], [r_cq], eng='dve')
            yield
            P.cp(cqf[:, 0, :, :], cq[:, 0, :, :], [r_cq], [r_cq], eng='dve')
            yield
            P.stt(cqf[:, 1, :, :], cqf[:, 0, :, :], -16.0, cqf[:, 1, :, :], ALU.mult, ALU.add, [r_cq], [r_cq])
            yield
            r_eq = r_sc2s
            for p2 in range(2):
                P.tt(eq[:, :, :, :], iota16[:, :].unsqueeze(1).unsqueeze(1).broadcast_to([128, 8, 16, 16]),
                     cqf[:, p2, :, :].unsqueeze(3).broadcast_to([128, 8, 16, 16]), ALU.is_equal,
                     [r_c, r_cq] + r_cvs, r_eq)
                yield
                P.tt(eq[:, :, :, :], eq[:, :, :, :], sifr[:, :, p2, :].unsqueeze(2).broadcast_to([128, 8, 16, 16]),
                     ALU.mult, r_eq + [r_sif], r_eq)
                yield
                P.op('dve', lambda e, p2=p2: e.tensor_reduce(
                    out=idx[:, p2, :], in_=eq[:, :, :, :].rearrange('p h k i -> p (h k) i'), axis=AX.X, op=ALU.add),
                    r_eq, [r_idx])
                yield
            P.stt(eidf[:, :], idx[:, 0, :], 128.0, idx[:, 1, :], ALU.mult, ALU.add, [r_idx], [r_eidf])
            yield
            P.cp(eid[s][:, :], eidf[:, :], [r_eidf], [r_eid[s]], eng='dve')
            yield
            gt_ = gate[s]
            P.tt(gt_[:, :, :], cv[:, :, :], cv[:, :, 0:1].broadcast_to([128, 8, 16]), ALU.subtract, r_cvs, [r_gate[s]])
            yield
            P.act(gt_[:, :, :], gt_[:, :, :], AF.Exp, [r_gate[s]], [r_gate[s]])
            yield
            P.op('dve', lambda e: e.tensor_reduce(out=gsum[:, :], in_=gt_[:, :, :], axis=AX.X, op=ALU.add),
                 [r_gate[s]], [r_gsum])
            yield
            P.op('dve', lambda e: e.reciprocal(gsum[:, :], gsum[:, :]), [r_gsum], [r_gsum])
            yield
            P.tt(gt_[:, :, :], gt_[:, :, :], gsum[:, :].unsqueeze(2).broadcast_to([128, 8, 16]), ALU.mult,
                 [r_gate[s], r_gsum], [r_gate[s]])
            yield

        slot_ctr = [0]
        dg_ctr = [0]
        NPB = 3
        prod = [sbuf(nc, st, 'prod%d' % i, [128, 1024], BF16) for i in range(NPB)]
        r_prod = P.Rs('prod', NPB)
        NJA = 3
        junkAs = [sbuf(nc, st, 'junkA%d' % i, [128, 1024], BF16) for i in range(NJA)]
        r_junkA = P.Rs('junkA', NJA)
        ja_ctr = [0]
        pb_ctr = [0]
        r_actc = [P.Rs('actc%d_' % i, 128) for i in range(2)]

        def experts(tt, rgen=None, pull=2):
            s = tt % 2
            yb = [4 + 2 * s, 5 + 2 * s]
            gflat = gate[s][:, :, :].rearrange('p h k -> p (h k)')
            for g in range(128 // NG):
                c0 = g * NG
                slots = []
                for j in range(NG):
                    hk = c0 + j
                    sl = slot_ctr[0] % NS
                    slot_ctr[0] += 1
                    slots.append(sl)
                    if not DBG['nogather']:
                        P.op('pool', lambda e, sl=sl, hk=hk, s=s: e.indirect_dma_start(
                            out=uvg[sl][:, :], out_offset=None, in_=uvb_d,
                            in_offset=bass.IndirectOffsetOnAxis(ap=eid[s][:, hk:hk + 1], axis=0)),
                            [r_eid[s]], [r_uv[sl]], dma=True)
                for j in range(NG):
                    hk = c0 + j
                    sl = slots[j]
                    if SPLIT_DOTS and (j % 2 == 1):
                        pb = pb_ctr[0] % NPB
                        pb_ctr[0] += 1
                        P.tt(prod[pb][:, :], xb[s][:, :], uvg[sl][:, 0:1024], ALU.mult, [r_xb[s], r_uv[sl]],
                             [r_prod[pb]])
                        ja = ja_ctr[0] % NJA
                        ja_ctr[0] += 1
                        P.op('act', lambda e, pb=pb, hk=hk, s=s, ja=ja: e.activation(
                            junkAs[ja][:, :], prod[pb][:, :], AF.Copy, accum_out=actv[s][:, hk:hk + 1]),
                            [r_prod[pb]], [r_actc[s][hk], r_junkA[ja]])
                    else:
                        jk = jk_ctr[0] % NJ
                        jk_ctr[0] += 1
                        P.op('dve', lambda e, sl=sl, hk=hk, s=s, jk=jk: e.scalar_tensor_tensor(
                            junks[jk][:, :], xb[s][:, :], 1.0, uvg[sl][:, 0:1024], ALU.mult, ALU.mult,
                            accum_out=actv[s][:, hk:hk + 1]),
                            [r_xb[s], r_uv[sl]], [r_actc[s][hk], r_junk[jk]])
                    if rgen is not None:
                        for _ in range(pull):
                            next(rgen, None)
                a_ = actv[s][:, c0:c0 + NG]
                t_ = gtmp[s][:, c0:c0 + NG]
                ra, rt, rc = r_actc[s][c0:c0 + NG], r_gt[s][g], r_coef[s][g]
                P.act(t_, a_, AF.Square, ra, [rt])
                P.ts(t_, t_, 0.044715, 1.0, ALU.mult, ALU.add, [rt], [rt])
                P.tt(t_, t_, a_, ALU.mult, [rt] + ra, [rt])
                P.act(t_, t_, AF.Sigmoid, [rt], [rt], scale=GELU_K)
                P.tt(t_, t_, a_, ALU.mult, [rt] + ra, [rt])
                P.tt(coef[s][:, c0:c0 + NG], t_, gflat[:, c0:c0 + NG], ALU.mult, [rt, r_gate[s]], [rc])
                for j in range(NG):
                    hk = c0 + j
                    sl = slots[j]
                    d = dg_ctr[0] % NDG
                    dg_ctr[0] += 1
                    P.act(dg[d][:, :], ident[:, :], AF.Copy, [r_c, rc], [r_dg[d]], scale=coef[s][:, hk:hk + 1])
                    for half in range(2):
                        P.mm(ps[yb[half]][:, :], dg[d][:, :], uvg[sl][:, 1024 + half * 512:1024 + (half + 1) * 512],
                             hk == 0, hk == 127, [r_dg[d], r_uv[sl]], [psr[yb[half]]])
            for half in range(2):
                P.stt(zt[s][:, half * 512:(half + 1) * 512], xt[s][:, half * 512:(half + 1) * 512], ALPHA,
                      ps[yb[half]][:, :], ALU.mult, ALU.add, [r_xt[s], psr[yb[half]]], [r_z[s]])
            ln_store(C, L, zt[s], r_z[s], x_out[tt * 128:(tt + 1) * 128, :])

        for _ in routing(0):
            pass
        for tt in range(NT):
            rgen = routing(tt + 1) if tt + 1 < NT else None
            experts(tt, rgen)
            if rgen is not None:
                for _ in rgen:
                    pass
        P.flush()


def peer_host_layout(inp, layer):
    wq = inp['peer_w_q'][layer]
    wq_l = np.ascontiguousarray(wq.reshape(8, 128, 16, 128).transpose(2, 1, 0, 3))
    keys = inp['peer_sub_keys'][layer].reshape(16, 128, 128)
    keysT = np.ascontiguousarray(keys.transpose(2, 0, 1))
    return {'wq': wq_l, 'keysT': keysT}


CH = 64
NCH = T // CH
RWKV_GN_EPS = 64e-5
DECAY_K = -math.exp(-0.5)


def phase_rwkv(C, x_in, x_out, W, S, stages=('A', 'A2', 'B', 'C'), nch_dbg=None):
    nc, P = C.nc, C.P
    ps, psr = C.ps, C.psr
    pi = 0

    def nb():
        nonlocal pi
        b = pi
        pi = (pi + 1) % 8
        return b

    QT = 512
    NQ = T // QT
    if 'A' in stages:
      with ExitStack() as st:
          ident = sbuf(nc, st, 'ident', [128, 128])
          xTh = sbuf(nc, st, 'xTh', [128, 8, T + 1])
          xl = [sbuf(nc, st, 'xl%d' % i, [128, 1024]) for i in range(2)]
          xm = [sbuf(nc, st, 'xm%d' % i, [128, 8, QT], BF16) for i in range(2)]
          wbig = [sbuf(nc, st, 'wbig%d' % i, [128, 8192]) for i in range(1)]
          wbb = [sbuf(nc, st, 'wbb%d' % i, [128, 8192], BF16) for i in range(2)]
          w2f = sbuf(nc, st, 'w2f', [128, 1024])
          r_wf, r_w2f = P.R('awf'), P.R('aw2f')
          stg = [sbuf(nc, st, 'stg%d' % i, [128, 512]) for i in range(4)]
          t1 = sbuf(nc, st, 't1', [128, QT], BF16)
          w2s = sbuf(nc, st, 'w2s', [128, 1024], BF16)
          cst = sbuf(nc, st, 'cstA', [128, 20, 8])
          r_c, r_xT = P.R('ac'), P.R('axT')
          r_xl = P.Rs('axl', 2)
          r_xm = P.Rs('axm', 2)
          r_wb = P.Rs('awb', 2)
          r_stg = P.Rs('astg', 4)
          r_t1, r_w2 = P.R('at1'), P.R('aw2')
          r_d = {k: P.R('ad_' + k) for k in ('r', 'k', 'lw', 'a', 'v', 'g')}
          P.dma(ident[:, :], W['ident'], [], [r_c])
          P.dma(cst[:, 0:6, :], W['rw_mix'], [], [r_c])
          P.dma(cst[:, 12:14, :], W['rw_w0a0'], [], [r_c])
          P.ts(cst[:, 6:12, :], cst[:, 0:6, :], -1.0, 1.0, ALU.mult, ALU.add, [r_c], [r_c])
          P.op('dve', lambda e: e.memset(xTh[:, :, 0:1], 0.0), [], [r_xT])
          for tt in range(T // 128):
              s = tt % 2
              P.dma(xl[s][:, :], x_in[tt * 128:(tt + 1) * 128, :], [], [r_xl[s]])
              for half in range(2):
                  b = nb()
                  for j in range(4):
                      k = half * 4 + j
                      P.tr(ps[b][:, j * 128:(j + 1) * 128], xl[s][:, k * 128:(k + 1) * 128], ident[:, :],
                           [r_xl[s], r_c], [psr[b]])
                  P.cp(xTh[:, half * 4:half * 4 + 4, 1 + tt * 128:1 + (tt + 1) * 128],
                       ps[b][:, :].rearrange('p (a b) -> p a b', a=4), [psr[b]], [r_xT],
                       eng=('act' if half else 'dve'))
          sti = 0
          xmi = 0
          for mi, m in enumerate(('r', 'k', 'lw', 'a', 'v', 'g')):
              wf = wbig[0]
              wb = wbb[mi % 2]
              rwb = r_wb[mi % 2]
              if m in ('r', 'k'):
                  P.dma(wf[:, :].rearrange('p (c k j) -> p c k j', c=8, k=8), W['rw_w' + m], [], [r_wf])
                  P.cp(wb[:, 0:4096], wf[:, 0:4096], [r_wf], [rwb], eng='act')
                  P.cp(wb[:, 4096:8192], wf[:, 4096:8192], [r_wf], [rwb], eng='dve')
              elif m in ('lw', 'a'):
                  P.dma(wf[:, 0:512].rearrange('p (k j) -> p k j', k=8), W['rw_%s1' % ('w' if m == 'lw' else 'a')],
                        [], [r_wf])
                  P.cp(wb[:, 0:512], wf[:, 0:512], [r_wf], [rwb], eng='act')
                  P.dma(w2f[0:64, :], W['rw_%s2' % ('w' if m == 'lw' else 'a')], [], [r_w2f])
                  P.cp(w2s[0:64, :], w2f[0:64, :], [r_w2f], [r_w2], eng='act')
              elif m == 'v':
                  P.dma(wf[:, :].rearrange('p (k j) -> p k j', k=8), W['rw_wv'], [], [r_wf])
                  P.cp(wb[:, 0:4096], wf[:, 0:4096], [r_wf], [rwb], eng='act')
                  P.cp(wb[:, 4096:8192], wf[:, 4096:8192], [r_wf], [rwb], eng='dve')
              else:
                  P.dma(wf[:, 0:1024].rearrange('p (k j) -> p k j', k=8), W['rw_g1'], [], [r_wf])
                  P.cp(wb[:, 0:1024], wf[:, 0:1024], [r_wf], [rwb], eng='act')
                  P.dma(w2f[:, :], W['rw_g2'], [], [r_w2f])
                  P.cp(w2s[:, :], w2f[:, :], [r_w2f], [r_w2], eng='act')
              for tq in range(NQ):
                  t0 = tq * QT
                  xs = xm[xmi % 2]
                  rxs = r_xm[xmi % 2]
                  xmi += 1
                  for k in range(8):
                      P.act(xs[:, k, :], xTh[:, k, t0:t0 + QT], AF.Copy, [r_xT, r_c], [rxs],
                            scale=cst[:, mi, k:k + 1])
                      P.stt(xs[:, k, :], xTh[:, k, t0 + 1:t0 + 1 + QT], cst[:, 6 + mi, k:k + 1], xs[:, k, :],
                            ALU.mult, ALU.add, [r_xT, r_c, rxs], [rxs])
                  if m in ('r', 'k'):
                      wv4 = wb[:, :].rearrange('p (c k j) -> p c k j', c=8, k=8)
                      for c in range(8):
                          b = nb()
                          for k in range(8):
                              P.mm(ps[b][:, :], wv4[:, c, k, :], xs[:, k, :], k == 0, k == 7, [rwb, rxs], [psr[b]])
                          sg = sti % 4
                          sti += 1
                          P.cp(stg[sg][:, :], ps[b][:, :], [psr[b]], [r_stg[sg]], eng=('act' if c % 2 else 'dve'))
                          P.dma(S[m][c * 128:(c + 1) * 128, t0:t0 + QT], stg[sg][:, :], [r_stg[sg]], [r_d[m]], q='pool')
                  elif m in ('lw', 'a'):
                      w1v = wb[:, 0:512].rearrange('p (k j) -> p k j', k=8)
                      b = nb()
                      for k in range(8):
                          P.mm(ps[b][0:64, :], w1v[:, k, :], xs[:, k, :], k == 0, k == 7, [rwb, rxs], [psr[b]])
                      if m == 'lw':
                          P.act(t1[0:64, :], ps[b][0:64, :], AF.Tanh, [psr[b]], [r_t1])
                      else:
                          P.cp(t1[0:64, :], ps[b][0:64, :], [psr[b]], [r_t1], eng='act')
                      for c in range(8):
                          b = nb()
                          P.mm(ps[b][:, :], w2s[0:64, c * 128:(c + 1) * 128], t1[0:64, :], True, True,
                               [r_w2, r_t1], [psr[b]])
                          sg = sti % 4
                          sti += 1
                          P.act(stg[sg][:, :], ps[b][:, :], AF.Sigmoid, [psr[b], r_c], [r_stg[sg]],
                                bias=cst[:, 12 + (0 if m == 'lw' else 1), c:c + 1])
                          if m == 'lw':
                              P.ts(stg[sg][:, :], stg[sg][:, :], DECAY_K, None, ALU.mult, None, [r_stg[sg]], [r_stg[sg]])
                          P.dma(S[m][c * 128:(c + 1) * 128, t0:t0 + QT], stg[sg][:, :], [r_stg[sg]], [r_d[m]], q='pool')
                  elif m == 'v':
                      wv3 = wb[:, :].rearrange('p (k j) -> p k j', k=8)
                      for tt in range(4):
                          for half in range(2):
                              b = nb()
                              for k in range(8):
                                  P.mm(ps[b][:, :], xs[:, k, tt * 128:(tt + 1) * 128],
                                       wv3[:, k, half * 512:(half + 1) * 512], k == 0, k == 7, [rwb, rxs], [psr[b]])
                              sg = sti % 4
                              sti += 1
                              P.cp(stg[sg][:, :], ps[b][:, :], [psr[b]], [r_stg[sg]], eng=('act' if half else 'dve'))
                              P.dma(S['v'][t0 + tt * 128:t0 + (tt + 1) * 128, half * 512:(half + 1) * 512],
                                    stg[sg][:, :], [r_stg[sg]], [r_d[m]], q='pool')
                  else:
                      g1v = wb[:, 0:1024].rearrange('p (k j) -> p k j', k=8)
                      b = nb()
                      for k in range(8):
                          P.mm(ps[b][:, :], g1v[:, k, :], xs[:, k, :], k == 0, k == 7, [rwb, rxs], [psr[b]])
                      P.act(t1[:, :], ps[b][:, :], AF.Sigmoid, [psr[b]], [r_t1])
                      for tt in range(4):
                          for half in range(2):
                              b = nb()
                              P.mm(ps[b][:, :], t1[:, tt * 128:(tt + 1) * 128], w2s[:, half * 512:(half + 1) * 512],
                                   True, True, [r_t1, r_w2], [psr[b]])
                              sg = sti % 4
                              sti += 1
                              P.cp(stg[sg][:, :], ps[b][:, :], [psr[b]], [r_stg[sg]], eng=('act' if half else 'dve'))
                              P.dma(S['g'][t0 + tt * 128:t0 + (tt + 1) * 128, half * 512:(half + 1) * 512],
                                    stg[sg][:, :], [r_stg[sg]], [r_d[m]], q='pool')
          P.flush()

    if 'A2' in stages:
      with ExitStack() as st:
          cst = sbuf(nc, st, 'cstB', [128, 5, 8])
          bones = sbuf(nc, st, 'bones', [128, 128])
          hsel = sbuf(nc, st, 'hsel', [128, 2])
          rmask = sbuf(nc, st, 'rmask', [128, QT])
          names = ('r', 'k', 'lw', 'a')
          tin = {n: [sbuf(nc, st, 'in_%s%d' % (n, i), [128, QT]) for i in range(2)] for n in names}
          r_in = {n: P.Rs('bin_' + n, 2) for n in names}
          tmps = {n: sbuf(nc, st, 'tmp_' + n, [128, QT]) for n in ('kkr', 'sq', 'nrm', 'kp', 'cl', 'eP', 'eN', 'ePm')}
          r_t = {n: P.R('bt_' + n) for n in tmps}
          onames = ('rb', 'kb', 'bb', 'ab')
          tout = {n: [sbuf(nc, st, 'o_%s%d' % (n, i), [128, QT], BF16) for i in range(2)] for n in onames}
          r_out = {n: P.Rs('bo_' + n, 2) for n in onames}
          rkr = sbuf(nc, st, 'rkr', [128, QT])
          r_rkr = P.R('rkr')
          rho_st = sbuf(nc, st, 'rho_st', [128, 4, 16])
          r_rho = P.R('rho')
          gcs = [sbuf(nc, st, 'gcs%d' % i, [128, 8]) for i in range(2)]
          r_gcs = P.Rs('gcs', 2)
          r_c = P.R('bc')
          r_dd = P.R('bdram')
          P.dma(cst[:, 0:3, :], W['rw_kkr'], [], [r_c])
          P.dma(bones[:, :], W['bones'], [], [r_c])
          P.dma(hsel[:, :], W['hsel'], [], [r_c])
          P.dma(rmask[:, :], W['rmask'], [], [r_c])
          P.ts(cst[:, 3, :], cst[:, 1, :], -1.0, 1.0, ALU.mult, ALU.add, [r_c], [r_c])
          it = 0
          for tq in range(NQ):
              t0 = tq * QT
              for c in range(8):
                  s = it % 2
                  it += 1
                  for n in names:
                      P.dma(tin[n][s][:, :], S[n][c * 128:(c + 1) * 128, t0:t0 + QT], [], [r_in[n][s]])
                  r_, k_, lw_, a_ = [tin[n][s] for n in names]
                  rr_, rk_, rlw_, ra_ = [r_in[n][s] for n in names]
                  T_ = tmps
                  P.act(T_['kkr'][:, :], k_[:, :], AF.Copy, [rk_, r_c], [r_t['kkr']], scale=cst[:, 0, c:c + 1])
                  P.act(T_['sq'][:, :], T_['kkr'][:, :], AF.Square, [r_t['kkr']], [r_t['sq']])
                  b = nb()
                  P.mm(ps[b][:, :], bones[:, :], T_['sq'][:, :], True, True, [r_c, r_t['sq']], [psr[b]])
                  P.act(T_['nrm'][:, :], ps[b][:, :], AF.Sqrt, [psr[b]], [r_t['nrm']])
                  P.ts(T_['nrm'][:, :], T_['nrm'][:, :], 1e-12, None, ALU.max, None, [r_t['nrm']], [r_t['nrm']])
                  P.op('dve', lambda e: e.reciprocal(T_['nrm'][:, :], T_['nrm'][:, :]), [r_t['nrm']], [r_t['nrm']])
                  P.tt(T_['kkr'][:, :], T_['kkr'][:, :], T_['nrm'][:, :], ALU.mult, [r_t['kkr'], r_t['nrm']],
                       [r_t['kkr']])
                  P.ts(T_['kp'][:, :], a_[:, :], cst[:, 1, c:c + 1], cst[:, 3, c:c + 1], ALU.mult, ALU.add,
                       [ra_, r_c], [r_t['kp']])
                  P.tt(T_['kp'][:, :], T_['kp'][:, :], k_[:, :], ALU.mult, [r_t['kp'], rk_], [r_t['kp']])
                  P.op('dve', lambda e, lw_=lw_: e.tensor_tensor_scan(T_['cl'][:, :], rmask[:, :], lw_[:, :], 0.0,
                                                                       ALU.mult, ALU.add),
                       [r_c, rlw_], [r_t['cl']])
                  P.act(T_['eP'][:, :], T_['cl'][:, :], AF.Exp, [r_t['cl']], [r_t['eP']])
                  P.act(T_['eN'][:, :], T_['cl'][:, :], AF.Exp, [r_t['cl']], [r_t['eN']], scale=-1.0)
                  P.tt(T_['cl'][:, :], T_['cl'][:, :], lw_[:, :], ALU.subtract, [r_t['cl'], rlw_, r_t['eP'], r_t['eN']],
                       [r_t['cl']])
                  P.act(T_['ePm'][:, :], T_['cl'][:, :], AF.Exp, [r_t['cl']], [r_t['ePm']])
                  o = {n: tout[n][s] for n in onames}
                  ro = {n: r_out[n][s] for n in onames}
                  P.tt(o['rb'][:, :], r_[:, :], T_['eP'][:, :], ALU.mult, [rr_, r_t['eP']], [ro['rb']])
                  P.tt(o['kb'][:, :], T_['kp'][:, :], T_['eN'][:, :], ALU.mult, [r_t['kp'], r_t['eN']], [ro['kb']])
                  P.tt(T_['sq'][:, :], T_['kkr'][:, :], a_[:, :], ALU.mult, [r_t['kkr'], ra_], [r_t['sq']])
                  P.tt(o['bb'][:, :], T_['sq'][:, :], T_['eN'][:, :], ALU.mult, [r_t['sq'], r_t['eN']], [ro['bb']])
                  P.stt(o['ab'][:, :], T_['kkr'][:, :], -1.0, T_['ePm'][:, :], ALU.mult, ALU.mult,
                        [r_t['kkr'], r_t['ePm']], [ro['ab']])
                  for n in onames:
                      P.dma(S[n][c * 128:(c + 1) * 128, t0:t0 + QT], o[n][:, :], [ro[n]], [r_dd], q='pool')
                  gs = it % 2
                  P.cp(gcs[gs][:, :], T_['eP'][:, :].rearrange('p (c t) -> p c t', t=CH)[:, :, CH - 1], [r_t['eP']],
                       [r_gcs[gs]], eng='act')
                  P.dma(S['gC'][c * 128:(c + 1) * 128, tq * 8:(tq + 1) * 8], gcs[gs][:, :], [r_gcs[gs]], [r_dd], q='pool')
                  P.stt(rkr[:, :], r_[:, :], cst[:, 2, c:c + 1], T_['kp'][:, :], ALU.mult, ALU.mult,
                        [rr_, r_c, r_t['kp']], [r_rkr])
                  b = nb()
                  for tt in range(4):
                      P.mm(ps[b][:, tt * 2:tt * 2 + 2], rkr[:, tt * 128:(tt + 1) * 128], hsel[:, :], True, True,
                           [r_rkr, r_c], [psr[b]])
                  P.cp(rho_st[:, :, 2 * c:2 * c + 2], ps[b][:, 0:8].rearrange('p (a b) -> p a b', b=2), [psr[b]],
                       [r_rho], eng='act')
              P.dma(S['rho'][t0:t0 + QT, :].rearrange('(a p) h -> p a h', p=128), rho_st[:, :, :], [r_rho], [r_dd], q='pool')
          P.flush()

    if 'B' in stages:
      with ExitStack() as st:
          HH = 8
          mSU = sbuf(nc, st, 'mSU', [64, HH, 64])
          mSL = sbuf(nc, st, 'mSL', [64, HH, 64])
          mUI = sbuf(nc, st, 'mUI', [64, HH, 64])
          mI = sbuf(nc, st, 'mI', [64, HH, 64])
          id64 = sbuf(nc, st, 'id64', [64, 64])
          gC = sbuf(nc, st, 'gC', [64, 16, NCH])
          lnxg = sbuf(nc, st, 'lnxg', [64, 1024])
          lnxb = sbuf(nc, st, 'lnxb', [64, 1024])
          r_c = P.R('sc')
          P.dma(mSU[:, :, :], W['mSU'], [], [r_c])
          P.dma(mSL[:, :, :], W['mSL'], [], [r_c])
          P.dma(mUI[:, :, :], W['mUI'], [], [r_c])
          P.dma(mI[:, :, :], W['mI'], [], [r_c])
          P.dma(id64[:, :], W['ident'][0:64, 0:64], [], [r_c])
          P.dma(gC[:, :, :], S['gC'].rearrange('(h k) c -> k h c', k=64), [], [r_c])
          P.dma(lnxg[:, :], W['rw_lnxg'][0:64, :], [], [r_c])
          P.dma(lnxb[:, :], W['rw_lnxb'][0:64, :], [], [r_c])
          lnames = ('ab', 'bb', 'kb', 'rb')
          lin = {n: [sbuf(nc, st, 'l_%s%d' % (n, i), [64, 16, CH], BF16) for i in range(2)] for n in lnames}
          r_lin = {n: P.Rs('sl_' + n, 2) for n in lnames}
          vt = [sbuf(nc, st, 'vt%d' % i, [64, 1024]) for i in range(2)]
          gt = [sbuf(nc, st, 'gt%d' % i, [64, 1024]) for i in range(2)]
          rh = [sbuf(nc, st, 'rh%d' % i, [64, 16]) for i in range(2)]
          r_vt, r_gt, r_rh = P.Rs('svt', 2), P.Rs('sgt', 2), P.Rs('srh', 2)
          mats = ('N', 'NT', 'Aak', 'Abr', 'Akr', 'bbT', 'kbT', 'P0', 'P1', 'Ma', 'Mb', 'MTa', 'MTb')
          mt = [{n: sbuf(nc, st, 'm_%s%d' % (n, hf), [64, HH, 64], BF16) for n in mats} for hf in range(2)]
          r_mt = [{n: P.R('sm_%s%d' % (n, hf)) for n in mats} for hf in range(2)]
          Y = sbuf(nc, st, 'Y', [64, 16, 64], BF16)
          XT = sbuf(nc, st, 'XT', [64, 16, 64], BF16)
          r_Y, r_XT = P.Rs('sY', 2), P.Rs('sXT', 2)
          ST = [sbuf(nc, st, 'ST%d' % i, [64, 16, 64]) for i in range(2)]
          r_ST = [P.Rs('sST%d_' % i, 2) for i in range(2)]
          O = [sbuf(nc, st, 'O%d' % i, [64, 16, 64]) for i in range(2)]
          r_O = P.Rs('sO', 2)
          Oc = sbuf(nc, st, 'Oc', [64, 16, 64])
          r_Oc = P.R('sOc')
          stat = sbuf(nc, st, 'stat', [64, 4, 16])
          r_stat = P.R('sstat')
          r_yd = P.R('syd')
          P.op('dve', lambda e: e.memset(ST[0][:, :, :], 0.0), [], r_ST[0])
          STb = [sbuf(nc, st, 'STb%d' % i, [64, 16, 64], BF16) for i in range(2)]
          r_STb = [P.Rs('sSTb%d_' % i, 2) for i in range(2)]
          P.op('dve', lambda e: e.memset(STb[0][:, :, :], 0.0), [], r_STb[0])
          id64b = sbuf(nc, st, 'id64b', [64, 64], BF16)
          P.cp(id64b[:, :], id64[:, :], [r_c], [r_c])
          vtb = [sbuf(nc, st, 'vtb%d' % i, [64, 1024], BF16) for i in range(2)]
          r_vtb = P.Rs('svtb', 2)
          stmp = [sbuf(nc, st, 'stmp%d' % i, [64, HH, 64]) for i in range(2)]
          r_stmp = P.Rs('sstmp', 2)
          gcx = [sbuf(nc, st, 'gcx%d' % i, [64, 16, 64]) for i in range(2)]
          r_gcx = P.Rs('sgcx', 2)

          def f2(t):
              return t.rearrange('p a b -> p (a b)') if len(t.shape) == 3 else t[:, :, :].rearrange('p a b -> p (a b)')

          def prod(hf, lhs, rhs_, rl, rr2, evac, dst, rdst, extra_r=()):
              b = nb()
              for j in range(HH):
                  h = hf * HH + j
                  P.mm(ps[b][0:64, j * 64:(j + 1) * 64], lhs(h), rhs_(h), True, True, rl + rr2, [psr[b]])
              pv = ps[b][0:64, :]
              evac(pv, psr[b])

          for c in range(NCH if nch_dbg is None else nch_dbg):
              s = c % 2
              t0 = c * CH
              for n in lnames:
                  P.dma(lin[n][s][:, :, :], S[n][:, t0:t0 + CH].rearrange('(h k) t -> k h t', k=64), [], [r_lin[n][s]])
              P.dma(vt[s][:, :], S['v'][t0:t0 + CH, :], [], [r_vt[s]])
              P.dma(gt[s][:, :], S['g'][t0:t0 + CH, :], [], [r_gt[s]])
              P.dma(rh[s][:, :], S['rho'][t0:t0 + CH, :], [], [r_rh[s]])
              P.cp(gcx[s][:, :, :], gC[:, :, c:c + 1].broadcast_to([64, 16, 64]), [r_c], [r_gcx[s]], eng='act')
              ab, bb, kb, rb = [lin[n][s] for n in lnames]
              rab, rbb, rkb, rrb = [[r_lin[n][s]] for n in lnames]
              vt3 = vt[s][:, :].rearrange('p (h v) -> p h v', h=16)
              P.cp(vtb[s][:, :], vt[s][:, :], [r_vt[s]], [r_vtb[s]], eng='act')
              vb3 = vtb[s][:, :].rearrange('p (h v) -> p h v', h=16)
              STbo, STbn = STb[c % 2], STb[(c + 1) % 2]
              rSTbo, rSTbn = r_STb[c % 2], r_STb[(c + 1) % 2]
              STo, STn = ST[c % 2], ST[(c + 1) % 2]
              rSTo, rSTn = r_ST[c % 2], r_ST[(c + 1) % 2]
              for hf in range(2):
                  M_, R_ = mt[hf], r_mt[hf]

                  def ev_mask(dst, rd, mask, eng='dve'):
                      return lambda pv, pr: P.tt(f2(dst), pv, f2(mask), ALU.mult, [pr, r_c], [rd], eng=eng)

                  def ev_copy(dst, rd, eng='act'):
                      return lambda pv, pr: P.cp(f2(dst), pv, [pr], [rd], eng=eng)

                  prod(hf, lambda h: bb[:, h, :], lambda h: ab[:, h, :], rbb, rab, ev_mask(M_['N'], R_['N'], mSU), None, None)
                  prod(hf, lambda h: ab[:, h, :], lambda h: bb[:, h, :], rab, rbb, ev_mask(M_['NT'], R_['NT'], mSL), None, None)
                  prod(hf, lambda h: kb[:, h, :], lambda h: ab[:, h, :], rkb, rab, ev_mask(M_['Aak'], R_['Aak'], mSU), None, None)
                  prod(hf, lambda h: bb[:, h, :], lambda h: rb[:, h, :], rbb, rrb, ev_mask(M_['Abr'], R_['Abr'], mUI), None, None)
                  prod(hf, lambda h: kb[:, h, :], lambda h: rb[:, h, :], rkb, rrb, ev_mask(M_['Akr'], R_['Akr'], mUI), None, None)
                  prod(hf, lambda h: bb[:, h, :], lambda h: id64b[:, :], rbb, [r_c], ev_copy(M_['bbT'], R_['bbT']), None, None)
                  prod(hf, lambda h: kb[:, h, :], lambda h: id64b[:, :], rkb, [r_c], ev_copy(M_['kbT'], R_['kbT']), None, None)
                  P.tt(f2(M_['P0']), f2(M_['N']), f2(mI), ALU.add, [R_['N'], r_c], [R_['P0']])
                  Mc, MTc, rMc, rMTc = M_['N'], M_['NT'], R_['N'], R_['NT']
                  Pc, Pn, rPc, rPn = M_['P0'], M_['P1'], R_['P0'], R_['P1']
                  bufs = [('Ma', 'MTa'), ('Mb', 'MTb')]
                  for lvl in range(5):
                      mn, mtn = bufs[lvl % 2]
                      prod(hf, (lambda h, Mc=Mc: Mc[:, h % HH, :]), (lambda h, MTc=MTc: MTc[:, h % HH, :]), [rMc], [rMTc],
                           ev_copy(M_[mtn], R_[mtn], eng='act'), None, None)
                      if lvl < 4:
                          prod(hf, (lambda h, MTc=MTc: MTc[:, h % HH, :]), (lambda h, Mc=Mc: Mc[:, h % HH, :]), [rMTc], [rMc],
                               ev_copy(M_[mn], R_[mn], eng='dve'), None, None)
                      prod(hf, (lambda h, m2t=M_[mtn]: m2t[:, h % HH, :]), (lambda h, Pc=Pc: Pc[:, h % HH, :]), [R_[mtn]], [rPc],
                           (lambda pv, pr, Pc=Pc, Pn=Pn, rPc=rPc, rPn=rPn:
                            P.tt(f2(Pn), pv, f2(Pc), ALU.add, [pr, rPc], [rPn])), None, None)
                      Mc, MTc, rMc, rMTc = M_[mn], M_[mtn], R_[mn], R_[mtn]
                      Pc, Pn, rPc, rPn = Pn, Pc, rPn, rPc
                  M_['Pf'], R_['Pf'] = Pc, rPc
              for hf in range(2):
                  M_, R_ = mt[hf], r_mt[hf]
                  b = nb()
                  for j in range(HH):
                      h = hf * HH + j
                      P.mm(ps[b][0:64, j * 64:(j + 1) * 64], ab[:, h, :], STbo[:, h, :], True, False,
                           rab + [rSTbo[hf]], [psr[b]])
                      P.mm(ps[b][0:64, j * 64:(j + 1) * 64], M_['Aak'][:, j, :], vb3[:, h, :], False, True,
                           [R_['Aak'], r_vtb[s]], [psr[b]])
                  P.cp(f2(Y[:, hf * HH:(hf + 1) * HH, :]), ps[b][0:64, :], [psr[b]],
                       [r_Y[hf]], eng=('act' if hf else 'dve'))
              for hf in range(2):
                  M_, R_ = mt[hf], r_mt[hf]
                  b = nb()
                  for j in range(HH):
                      h = hf * HH + j
                      P.mm(ps[b][0:64, j * 64:(j + 1) * 64], M_['Pf'][:, j, :], Y[:, h, :], True, True,
                           [R_['Pf'], r_Y[hf]], [psr[b]])
                  P.cp(f2(XT[:, hf * HH:(hf + 1) * HH, :]), ps[b][0:64, :], [psr[b]],
                       [r_XT[hf]], eng=('act' if hf else 'dve'))
              for hf in range(2):
                  M_, R_ = mt[hf], r_mt[hf]
                  b = nb()
                  for j in range(HH):
                      h = hf * HH + j
                      o_ = ps[b][0:64, j * 64:(j + 1) * 64]
                      P.mm(o_, M_['bbT'][:, j, :], XT[:, h, :], True, False, [R_['bbT'], r_XT[hf]], [psr[b]])
                      P.mm(o_, M_['kbT'][:, j, :], vb3[:, h, :], False, True, [R_['kbT'], r_vtb[s]], [psr[b]])
                  P.tt(f2(stmp[hf]), ps[b][0:64, :], f2(STo[:, hf * HH:(hf + 1) * HH, :]), ALU.add,
                       [psr[b], rSTo[hf]], [r_stmp[hf]])
                  P.tt(f2(STn[:, hf * HH:(hf + 1) * HH, :]), f2(stmp[hf]), f2(gcx[s][:, hf * HH:(hf + 1) * HH, :]),
                       ALU.mult, [r_stmp[hf], r_gcx[s]], [rSTn[hf]])
                  P.cp(f2(STbn[:, hf * HH:(hf + 1) * HH, :]), f2(STn[:, hf * HH:(hf + 1) * HH, :]), [rSTn[hf]],
                       [rSTbn[hf]], eng='act')
                  b = nb()
                  for j in range(HH):
                      h = hf * HH + j
                      o_ = ps[b][0:64, j * 64:(j + 1) * 64]
                      P.mm(o_, rb[:, h, :], STbo[:, h, :], True, False, rrb + [rSTbo[hf]], [psr[b]])
                      P.mm(o_, M_['Abr'][:, j, :], XT[:, h, :], False, False, [R_['Abr'], r_XT[hf]], [psr[b]])
                      P.mm(o_, M_['Akr'][:, j, :], vb3[:, h, :], False, True, [R_['Akr'], r_vtb[s]], [psr[b]])
                  P.cp(f2(O[s][:, hf * HH:(hf + 1) * HH, :]), ps[b][0:64, :], [psr[b]],
                       [r_O[s]], eng='act')
              Os = O[s]
              P.op('dve', lambda e, Os=Os: e.tensor_reduce(out=stat[:, 0, :], in_=Os[:, :, :], axis=AX.X, op=ALU.add),
                   [r_O[s]], [r_stat])
              P.ts(stat[:, 0, :], stat[:, 0, :], 1.0 / 64.0, None, ALU.mult, None, [r_stat], [r_stat])
              P.tt(Oc[:, :, :], Os[:, :, :], stat[:, 0, :].unsqueeze(2).broadcast_to([64, 16, 64]), ALU.subtract,
                   [r_O[s], r_stat], [r_Oc])
              P.act(Os[:, :, :], Oc[:, :, :], AF.Square, [r_Oc], [r_O[s]])
              P.op('dve', lambda e, Os=Os: e.tensor_reduce(out=stat[:, 1, :], in_=Os[:, :, :], axis=AX.X, op=ALU.add),
                   [r_O[s]], [r_stat])
              P.ts(stat[:, 1, :], stat[:, 1, :], 1.0 / 64.0, RWKV_GN_EPS, ALU.mult, ALU.add, [r_stat], [r_stat])
              P.act(stat[:, 1, :], stat[:, 1, :], AF.Sqrt, [r_stat], [r_stat])
              P.op('dve', lambda e: e.reciprocal(stat[:, 1, :], stat[:, 1, :]), [r_stat], [r_stat])
              P.tt(Oc[:, :, :], Oc[:, :, :], stat[:, 1, :].unsqueeze(2).broadcast_to([64, 16, 64]), ALU.mult,
                   [r_Oc, r_stat], [r_Oc])
              Of = Oc[:, :, :].rearrange('p h v -> p (h v)')
              P.tt(Of, Of, lnxg[:, :], ALU.mult, [r_Oc, r_c], [r_Oc])
              P.tt(Of, Of, lnxb[:, :], ALU.add, [r_Oc, r_c], [r_Oc])
              P.tt(Os[:, :, :], vt3, rh[s][:, :].unsqueeze(2).broadcast_to([64, 16, 64]), ALU.mult,
                   [r_vt[s], r_rh[s], r_O[s]], [r_O[s]])
              P.tt(Of, Of, Os[:, :, :].rearrange('p h v -> p (h v)'), ALU.add, [r_Oc, r_O[s]], [r_Oc])
              P.tt(Of, Of, gt[s][:, :], ALU.mult, [r_Oc, r_gt[s]], [r_Oc])
              P.dma(S['y'][t0:t0 + CH, :], Of, [r_Oc], [r_yd], q='pool')
          P.flush()

    if 'C' in stages:
      with ExitStack() as st:
          ident = sbuf(nc, st, 'ident', [128, 128])
          wo = sbuf(nc, st, 'wo', [128, 8, 1024], BF16)
          wstg = [sbuf(nc, st, 'wostg%d' % i, [128, 1024]) for i in range(2)]
          r_wstg = P.Rs('cwstg', 2)
          yl = [sbuf(nc, st, 'yl%d' % i, [128, 1024]) for i in range(2)]
          xl = [sbuf(nc, st, 'xl%d' % i, [128, 1024]) for i in range(2)]
          yT = [sbuf(nc, st, 'yT%d' % i, [128, 8, 128], BF16) for i in range(2)]
          L = ln_setup(C, st, W['ln_g'], W['ln_b'], 'rw')
          L['q'] = 'pool'
          r_c = P.R('cc')
          r_yl, r_xl, r_yT = P.Rs('cyl', 2), P.Rs('cxl', 2), P.Rs('cyT', 2)
          P.dma(ident[:, :], W['ident'], [], [r_c])
          for k in range(8):
              P.dma(wstg[k % 2][:, :], W['rw_wo'][:, k, :], [], [r_wstg[k % 2]])
              P.cp(wo[:, k, :], wstg[k % 2][:, :], [r_wstg[k % 2]], [r_c], eng=('act' if k % 2 else 'dve'))
          for tt in range(T // 128 if nch_dbg is None else nch_dbg):
              s = tt % 2
              P.dma(yl[s][:, :], S['y'][tt * 128:(tt + 1) * 128, :], [], [r_yl[s]])
              P.dma(xl[s][:, :], x_in[tt * 128:(tt + 1) * 128, :], [], [r_xl[s]])
              for half in range(2):
                  b = nb()
                  for j in range(4):
                      k = half * 4 + j
                      P.tr(ps[b][:, j * 128:(j + 1) * 128], yl[s][:, k * 128:(k + 1) * 128], ident[:, :],
                           [r_yl[s], r_c], [psr[b]])
                  P.cp(yT[s][:, half * 4:half * 4 + 4, :].rearrange('p a b -> p (a b)'), ps[b][:, :],
                       [psr[b]], [r_yT[s]], eng=('act' if half else 'dve'))
              bs = [nb(), nb()]
              for half in range(2):
                  for k in range(8):
                      P.mm(ps[bs[half]][:, :], yT[s][:, k, :], wo[:, k, half * 512:(half + 1) * 512], k == 0, k == 7,
                           [r_yT[s], r_c], [psr[bs[half]]])
              for half in range(2):
                  P.stt(yl[s][:, half * 512:(half + 1) * 512], xl[s][:, half * 512:(half + 1) * 512], ALPHA,
                        ps[bs[half]][:, :], ALU.mult, ALU.add, [r_xl[s], psr[bs[half]], r_yT[s]], [r_yl[s]])
              ln_store(C, L, yl[s], r_yl[s], x_out[tt * 128:(tt + 1) * 128, :])
          P.flush()


def rwkv_scratch(nc, tag, kind='Internal'):
    S = {}
    for n in ('r', 'k', 'lw', 'a'):
        S[n] = nc.dram_tensor('rs_%s_%s' % (tag, n), [1024, T], F32, kind=kind).ap()
    for n in ('rb', 'kb', 'bb', 'ab'):
        S[n] = nc.dram_tensor('rs_%s_%s' % (tag, n), [1024, T], BF16, kind=kind).ap()
    for n in ('v', 'g', 'y'):
        S[n] = nc.dram_tensor('rs_%s_%s' % (tag, n), [T, 1024], F32, kind=kind).ap()
    S['gC'] = nc.dram_tensor('rs_%s_gC' % tag, [1024, NCH], F32, kind=kind).ap()
    S['rho'] = nc.dram_tensor('rs_%s_rho' % tag, [T, 16], F32, kind=kind).ap()
    return S


def rwkv_host_layout(inp):
    f = np.float32
    g = lambda n: inp[n][0]

    def lhsT_chunks(w, ncol_chunks):
        return np.ascontiguousarray(w.reshape(8, 128, ncol_chunks, 128).transpose(1, 2, 0, 3))

    def rows_pk(w):
        return np.ascontiguousarray(w.reshape(8, 128, -1).transpose(1, 0, 2))

    H = {}
    H['rw_wr'] = lhsT_chunks(g('rw_w_r'), 8)
    H['rw_wk'] = lhsT_chunks(g('rw_w_k'), 8)
    H['rw_wv'] = rows_pk(g('rw_w_v'))
    H['rw_wo'] = rows_pk(g('rw_w_o'))
    H['rw_w1'] = rows_pk(g('rw_w1'))
    H['rw_a1'] = rows_pk(g('rw_a1'))
    H['rw_g1'] = rows_pk(g('rw_g1'))
    H['rw_w2'] = np.ascontiguousarray(g('rw_w2'))
    H['rw_a2'] = np.ascontiguousarray(g('rw_a2'))
    H['rw_g2'] = np.ascontiguousarray(g('rw_g2'))
    mix = g('rw_mix')[[0, 2, 1, 4, 3, 5]]
    H['rw_mix'] = np.ascontiguousarray(mix.reshape(6, 8, 128).transpose(2, 0, 1))
    H['rw_w0a0'] = np.ascontiguousarray(np.stack([g('rw_w0'), g('rw_a0')]).reshape(2, 8, 128).transpose(2, 0, 1))
    H['rw_kkr'] = np.ascontiguousarray(
        np.stack([g('rw_k_k'), g('rw_k_a'), g('rw_r_k').reshape(-1)]).reshape(3, 8, 128).transpose(2, 0, 1))
    H['rw_lnxg'] = bc128(g('rw_lnx_g'))
    H['rw_lnxb'] = bc128(g('rw_lnx_b'))
    bones = np.zeros((128, 128), f)
    bones[:64, :64] = 1
    bones[64:, 64:] = 1
    H['bones'] = bones
    hsel = np.zeros((128, 2), f)
    hsel[:64, 0] = 1
    hsel[64:, 1] = 1
    H['hsel'] = hsel
    rm = np.ones((128, 512), f)
    rm[:, ::CH] = 0
    H['rmask'] = rm
    s_ = np.arange(64)[:, None]
    t_ = np.arange(64)[None, :]
    rep = lambda m: np.ascontiguousarray(np.broadcast_to(m.astype(f)[:, None, :], (64, 8, 64)))
    H['mSU'] = rep(s_ < t_)
    H['mSL'] = rep(s_ > t_)
    H['mUI'] = rep(s_ <= t_)
    H['mI'] = rep(s_ == t_)
    return H


FUSED = True
PHASES = ('rg', 'peer0', 'rwkv', 'peer1')


def host_arrays(inp, ph):
    ident = np.eye(128, dtype=np.float32)
    if ph == 'rg':
        H = rg_host_layout(inp)
        H['ln_g'] = bc128(inp['ln_g'][0, 0])
        H['ln_b'] = bc128(inp['ln_b'][0, 0])
    elif ph == 'rwkv':
        H = rwkv_host_layout(inp)
        H['ln_g'] = bc128(inp['ln_g'][1, 0])
        H['ln_b'] = bc128(inp['ln_b'][1, 0])
    else:
        layer = int(ph[-1])
        H = peer_host_layout(inp, layer)
        H['iota16'] = bc128(np.arange(16, dtype=np.float32))
        H['ln_g'] = bc128(inp['ln_g'][layer, 1])
        H['ln_b'] = bc128(inp['ln_b'][layer, 1])
        H['uv'] = np.ascontiguousarray(np.concatenate([inp['peer_u'][layer], inp['peer_v'][layer]], axis=1))
    H['ident'] = ident
    return {k: np.ascontiguousarray(v, dtype=np.float32) for k, v in H.items()}


def build_program(phs, HA):
    nc = bass.Bass("TRN2", target_bir_lowering=False)
    xin = nc.dram_tensor('xin', [T, D], F32, kind='ExternalInput').ap()
    xout = nc.dram_tensor('xout', [T, D], F32, kind='ExternalOutput').ap()
    Ws = {}
    for ph in phs:
        Ws[ph] = {k: nc.dram_tensor('%s_%s' % (ph, k), list(v.shape), F32, kind='ExternalInput').ap()
                  for k, v in HA[ph].items()}
    acts = [xin]
    for i in range(len(phs) - 1):
        acts.append(nc.dram_tensor('act%d' % i, [T, D], F32, kind='Internal').ap())
    acts.append(xout)
    with ExitStack() as st:
        C = make_ctx(nc, st)
        for i, ph in enumerate(phs):
            if ph == 'rg':
                phase_rg(C, acts[i], acts[i + 1], Ws[ph])
            elif ph == 'rwkv':
                S = rwkv_scratch(nc, ph)
                phase_rwkv(C, acts[i], acts[i + 1], Ws[ph], S)
            else:
                qT_d = nc.dram_tensor('qT_' + ph, [16, 128, T], F32, kind='Internal').ap()
                uvb_d = nc.dram_tensor('uvb_' + ph, [16384, 2048], BF16, kind='Internal').ap()
                phase_peer(C, acts[i], acts[i + 1], Ws[ph], qT_d, uvb_d, ph)
    return nc


def run_launch(phs, HA, x_cores):
    nc = build_program(phs, HA)
    base = {}
    for ph in phs:
        for k, v in HA[ph].items():
            base['%s_%s' % (ph, k)] = v
    in_maps = []
    for b in range(NCORES):
        m = dict(base)
        m['xin'] = np.ascontiguousarray(x_cores[b], dtype=np.float32)
        in_maps.append(m)
    res = run_bass_kernel_spmd(nc, in_maps, core_ids=list(range(NCORES)))
    return [np.asarray(res.results[b]['xout']) for b in range(NCORES)]


def kernel(**inputs):
    inp = {k: np.asarray(v) for k, v in inputs.items()}
    x = [inp['x'][b] for b in range(NCORES)]
    groups = [PHASES] if FUSED else [(p,) for p in PHASES]
    for phs in groups:
        HA = {ph: host_arrays(inp, ph) for ph in phs}
        x = run_launch(phs, HA, x)
    return np.stack(x, axis=0).astype(np.float32)
```
